# Optimizing a Trainium2 kernel written in Bass

```python
import math
import jax, jax.numpy as jnp
from jax import lax
import numpy as np

D_MODEL = 1024
BATCH = 4
SEQ = 8192
DEPTH = 4

N_META = 16
ATT_HEAD_DIM = 64
ATT_HEADS = D_MODEL // ATT_HEAD_DIM
ATT_KV_HEADS = ATT_HEADS // 8
ATT_WIDTH = ATT_HEADS * ATT_HEAD_DIM
ATT_KV_WIDTH = ATT_KV_HEADS * ATT_HEAD_DIM
WINDOW = 128
ATT_BLOCK = 128
ROT_DIM = ATT_HEAD_DIM // 4
ROPE_THETA = 500000.0
M_HEADS = 4
M_V_DIM = D_MODEL // M_HEADS
M_QK_DIM = M_V_DIM // 2
M_WIDTH = M_HEADS * M_V_DIM
M_QK_WIDTH = M_HEADS * M_QK_DIM
CHUNK = 64
CONV_K = 4
MIX_WIDTH = ATT_WIDTH + M_WIDTH
SPLIT_SIZES = (ATT_WIDTH, ATT_KV_WIDTH, ATT_KV_WIDTH, ATT_WIDTH,
               M_QK_WIDTH, M_QK_WIDTH, M_WIDTH, M_HEADS, M_HEADS, M_WIDTH, M_WIDTH)
IN_WIDTH = sum(SPLIT_SIZES)
EPS = 1e-6
NEG = -1e30

kernel_name = "hymba_swa_sink_mlstm_hybrid"


def split_points():
    pts, acc = [], 0
    for s in SPLIT_SIZES[:-1]:
        acc += s
        pts.append(acc)
    return pts


def rms_norm(x, g):
    xf = x.astype(jnp.float32)
    y = xf * lax.rsqrt(jnp.mean(xf * xf, axis=-1, keepdims=True) + EPS)
    return (y * g.astype(jnp.float32)).astype(x.dtype)


def rope_tables(length):
    pos = jnp.arange(length, dtype=jnp.float32)
    inv_freq = ROPE_THETA ** (-jnp.arange(0, ROT_DIM, 2, dtype=jnp.float32) / ROT_DIM)
    ang = pos[:, None] * inv_freq[None, :]
    return jnp.cos(ang), jnp.sin(ang)


def apply_partial_rope(x, cos, sin):
    half = ROT_DIM // 2
    x1, x2, rest = x[..., :half], x[..., half:ROT_DIM], x[..., ROT_DIM:]
    c = cos[None, :, None, :]
    s = sin[None, :, None, :]
    out = jnp.concatenate([x1 * c - x2 * s, x2 * c + x1 * s, rest.astype(jnp.float32)], axis=-1)
    return out.astype(x.dtype)


def sliding_window_sink_attention(q, k, v, sink):
    B, L, H, Dh = q.shape
    KV = k.shape[2]
    G = H // KV
    pad = ATT_BLOCK - N_META
    Lp = L + pad
    nb = Lp // ATT_BLOCK
    padf = lambda a: jnp.pad(a, ((0, 0), (pad, 0), (0, 0), (0, 0)))
    qb = padf(q).reshape(B, nb, ATT_BLOCK, KV, G, Dh)

    def band_keys(a):
        ab = padf(a).reshape(B, nb, ATT_BLOCK, KV, Dh)
        prev = jnp.concatenate([jnp.zeros_like(ab[:, :1]), ab[:, :-1]], axis=1)
        meta = jnp.broadcast_to(a[:, None, :N_META], (B, nb, N_META, KV, Dh))
        return jnp.concatenate([meta, prev, ab], axis=2)

    kb = band_keys(k)
    vb = band_keys(v)

    blk = jnp.arange(nb)
    qpos = blk[:, None] * ATT_BLOCK + jnp.arange(ATT_BLOCK)[None, :] - pad
    kpos = blk[:, None] * ATT_BLOCK - ATT_BLOCK + jnp.arange(2 * ATT_BLOCK)[None, :] - pad
    mpos = jnp.arange(N_META)
    valid_meta = mpos[None, None, :] <= qpos[:, :, None]
    dist = qpos[:, :, None] - kpos[:, None, :]
    valid_band = (kpos[:, None, :] >= N_META) & (dist >= 0) & (dist < WINDOW)
    mask = jnp.concatenate([valid_meta, valid_band], axis=-1)

    scale = 1.0 / math.sqrt(Dh)
    scores = jnp.einsum('bnqkgd,bnskd->bnkgqs', qb, kb).astype(jnp.float32) * scale
    scores = jnp.where(mask[None, :, None, None], scores, NEG)
    sink_b = sink.astype(jnp.float32).reshape(KV, G)[None, None, :, :, None, None]
    m = jnp.maximum(scores.max(axis=-1, keepdims=True), sink_b)
    e = jnp.exp(scores - m)
    probs = e / (e.sum(axis=-1, keepdims=True) + jnp.exp(sink_b - m))
    out = jnp.einsum('bnkgqs,bnskd->bnqkgd', probs.astype(vb.dtype), vb)
    return out.reshape(B, Lp, H, Dh)[:, pad:]


def causal_depthwise_conv(x, w, b):
    C = x.shape[-1]
    y = lax.conv_general_dilated(x, w[:, None, :].astype(x.dtype), (1,), [(CONV_K - 1, 0)],
                                 dimension_numbers=('NWC', 'WIO', 'NWC'), feature_group_count=C)
    return y + b.astype(x.dtype)


def mlstm_chunkwise(q, k, v, log_i, log_f):
    B, L, NH, DQK = q.shape
    DV = v.shape[-1]
    pad = CHUNK - N_META
    Lp = L + pad
    NC = Lp // CHUNK
    f32 = jnp.float32

    def prep(a):
        a = jnp.pad(a.astype(f32), ((0, 0), (pad, 0), (0, 0), (0, 0)))
        return a.reshape(B, NC, CHUNK, NH, a.shape[-1]).transpose(0, 3, 1, 2, 4)

    q, k, v = prep(q), prep(k), prep(v)
    log_i = jnp.pad(log_i, ((0, 0), (pad, 0), (0, 0)), constant_values=NEG)
    log_f = jnp.pad(log_f, ((0, 0), (pad, 0), (0, 0)), constant_values=0.0)
    log_i = log_i.reshape(B, NC, CHUNK, NH).transpose(0, 3, 1, 2)
    log_f = log_f.reshape(B, NC, CHUNK, NH).transpose(0, 3, 1, 2)

    b = jnp.cumsum(log_f, axis=-1)
    bT = b[..., -1]
    causal = jnp.tril(jnp.ones((CHUNK, CHUNK), dtype=bool))
    D = jnp.where(causal, b[..., :, None] - b[..., None, :] + log_i[..., None, :], NEG)

    g = bT[..., None] - b + log_i
    mg = g.max(axis=-1)
    wg = jnp.exp(g - mg[..., None])
    kv_chunk = jnp.einsum('bhcsk,bhcsv->bhckv', wg[..., None] * k, v)
    n_chunk = jnp.einsum('bhcs,bhcsk->bhck', wg, k)

    def step(carry, inp):
        C, n, m = carry
        kv_c, n_c, bT_c, mg_c = inp
        m_new = jnp.maximum(bT_c + m, mg_c)
        a = jnp.exp(bT_c + m - m_new)
        w = jnp.exp(mg_c - m_new)
        C_new = a[..., None, None] * C + w[..., None, None] * kv_c
        n_new = a[..., None] * n + w[..., None] * n_c
        return (C_new, n_new, m_new), (C, n, m)

    init = (jnp.zeros((B, NH, DQK, DV), f32), jnp.zeros((B, NH, DQK), f32), jnp.zeros((B, NH), f32))
    xs = (jnp.moveaxis(kv_chunk, 2, 0), jnp.moveaxis(n_chunk, 2, 0),
          jnp.moveaxis(bT, 2, 0), jnp.moveaxis(mg, 2, 0))
    _, (C_in, n_in, m_in) = lax.scan(step, init, xs)
    C_in = jnp.moveaxis(C_in, 0, 2)
    n_in = jnp.moveaxis(n_in, 0, 2)
    m_in = jnp.moveaxis(m_in, 0, 2)

    a_t = b + m_in[..., None]
    m_t = jnp.maximum(a_t, D.max(axis=-1))
    S = jnp.einsum('bhctd,bhcsd->bhcts', q, k) * jnp.exp(D - m_t[..., None])
    inter = jnp.exp(a_t - m_t)
    num = inter[..., None] * jnp.einsum('bhctk,bhckv->bhctv', q, C_in) \
        + jnp.einsum('bhcts,bhcsv->bhctv', S, v)
    den = inter * jnp.einsum('bhctk,bhck->bhct', q, n_in) + S.sum(axis=-1)
    h = num / jnp.maximum(jnp.abs(den), jnp.exp(-m_t))[..., None]
    h = h.transpose(0, 2, 3, 1, 4).reshape(B, Lp, NH, DV)
    return h[:, pad:]


def hybrid_layer(x, cos, sin, norm_g, w_in, q_norm_g, k_norm_g, sink,
                 conv_w, conv_b, b_i, b_f, out_norm_g, w_out):
    B, L, _ = x.shape
    h = rms_norm(x, norm_g)
    z = h @ w_in.astype(x.dtype)
    aq, ak, av, ag, mq, mk, mv, mi, mf, mo, mgate = jnp.split(z, split_points(), axis=-1)

    q = rms_norm(aq.reshape(B, L, ATT_HEADS, ATT_HEAD_DIM), q_norm_g)
    k = rms_norm(ak.reshape(B, L, ATT_KV_HEADS, ATT_HEAD_DIM), k_norm_g)
    q = apply_partial_rope(q, cos, sin)
    k = apply_partial_rope(k, cos, sin)
    v = av.reshape(B, L, ATT_KV_HEADS, ATT_HEAD_DIM)
    ya = sliding_window_sink_attention(q, k, v, sink).reshape(B, L, ATT_WIDTH)
    ya = ya * jax.nn.silu(ag)

    qk = jax.nn.silu(causal_depthwise_conv(jnp.concatenate([mq, mk], axis=-1), conv_w, conv_b))
    mq, mk = jnp.split(qk, [M_QK_WIDTH], axis=-1)
    mq = mq.reshape(B, L, M_HEADS, M_QK_DIM)
    mk = mk.reshape(B, L, M_HEADS, M_QK_DIM) * (M_QK_DIM ** -0.5)
    mv = mv.reshape(B, L, M_HEADS, M_V_DIM)
    log_i = (mi + b_i).astype(jnp.float32)
    log_f = jax.nn.log_sigmoid((mf + b_f).astype(jnp.float32))
    hm = mlstm_chunkwise(mq, mk, mv, log_i, log_f)
    hm = rms_norm(hm, out_norm_g.reshape(M_HEADS, M_V_DIM)).reshape(B, L, M_WIDTH).astype(x.dtype)
    ym = hm * jax.nn.sigmoid(mo) * jax.nn.silu(mgate)

    y = jnp.concatenate([ya, ym], axis=-1) @ w_out.astype(x.dtype)
    return x + y


def setup_inputs(seed: int = 0) -> dict:
    key = jax.random.key(seed)
    ks = jax.random.split(key, 13)
    f32 = jnp.float32
    n = lambda k, shape, s: jax.random.normal(k, shape, f32) * s
    return {
        "x": n(ks[0], (BATCH, SEQ, D_MODEL), 1.0),
        "meta": n(ks[1], (N_META, D_MODEL), 1.0),
        "norm_g": 1.0 + n(ks[2], (DEPTH, D_MODEL), 0.02),
        "w_in": n(ks[3], (DEPTH, D_MODEL, IN_WIDTH), D_MODEL ** -0.5),
        "attn_q_norm_g": 1.0 + n(ks[4], (DEPTH, ATT_HEAD_DIM), 0.02),
        "attn_k_norm_g": 1.0 + n(ks[5], (DEPTH, ATT_HEAD_DIM), 0.02),
        "attn_sink": n(ks[6], (DEPTH, ATT_HEADS), 0.5),
        "mlstm_conv_w": n(ks[7], (DEPTH, CONV_K, 2 * M_QK_WIDTH), CONV_K ** -0.5),
        "mlstm_conv_b": n(ks[8], (DEPTH, 2 * M_QK_WIDTH), 0.02),
        "mlstm_b_i": n(ks[9], (DEPTH, M_HEADS), 0.1),
        "mlstm_b_f": jnp.linspace(3.0, 6.0, M_HEADS, dtype=f32)[None, :] + n(ks[10], (DEPTH, M_HEADS), 0.1),
        "mlstm_out_norm_g": 1.0 + n(ks[11], (DEPTH, M_WIDTH), 0.02),
        "w_out": n(ks[12], (DEPTH, MIX_WIDTH, D_MODEL), 0.5 * MIX_WIDTH ** -0.5),
    }


def reference(x, meta, norm_g, w_in, attn_q_norm_g, attn_k_norm_g, attn_sink,
              mlstm_conv_w, mlstm_conv_b, mlstm_b_i, mlstm_b_f, mlstm_out_norm_g, w_out):
    B = x.shape[0]
    h = jnp.concatenate([jnp.broadcast_to(meta.astype(x.dtype)[None], (B, N_META, x.shape[-1])), x], axis=1)
    L = h.shape[1]
    cos, sin = rope_tables(L)
    for l in range(DEPTH):
        h = hybrid_layer(h, cos, sin, norm_g[l], w_in[l], attn_q_norm_g[l], attn_k_norm_g[l],
                         attn_sink[l], mlstm_conv_w[l], mlstm_conv_b[l], mlstm_b_i[l],
                         mlstm_b_f[l], mlstm_out_norm_g[l], w_out[l])
    return h[:, N_META:]
```

```python
import math
import numpy as np
from contextlib import ExitStack
import concourse.bass as bass
import concourse.mybir as mybir
from concourse.bass_utils import run_bass_kernel_spmd

F32 = mybir.dt.float32
BF16 = mybir.dt.bfloat16
AF = mybir.ActivationFunctionType
ALU = mybir.AluOpType
AX = mybir.AxisListType

D_MODEL = 1024
N_META = 16
IN_W = 6408
EPS = 1e-6
NEGM = -30000.0
LN_HALF = math.log(0.5)
LN_CK = math.log(0.5 / math.sqrt(128.0))

SEGS = [(0, 512, 0), (512, 512, 512), (1024, 256, 1024), (4352, 8, 1280), (1280, 512, 1288), (1792, 512, 1800),
        (2304, 512, 2312), (2816, 512, 2824), (3328, 512, 3336), (3840, 512, 3848),
        (4360, 512, 4360), (4872, 512, 4872), (5384, 512, 5384), (5896, 512, 5896)]
SEG_OF = {"q0": [0], "q1": [1], "kv": [2, 3], "ag0": [4], "ag1": [5], "mq": [6], "mk": [7], "mv0": [8], "mv1": [9],
          "mo0": [10], "mo1": [11], "mg0": [12], "mg1": [13]}


class Buf:
    __slots__ = ("name", "w", "r")

    def __init__(self, name):
        self.name = name
        self.w = None
        self.r = {}


class Sched:
    NDMA = 24

    def __init__(self, nc, es):
        self.nc = nc
        self.engs = {"pe": nc.tensor, "act": nc.scalar, "dve": nc.vector, "pool": nc.gpsimd, "sp": nc.sync}
        self.sems = {}
        self.cnt = {}
        for k in self.engs:
            self.sems[k] = es.enter_context(nc.semaphore("s_" + k))
            self.cnt[k] = 0
        self.dsems = [es.enter_context(nc.semaphore(f"s_dma{i}")) for i in range(self.NDMA)]
        self.dval = [0] * self.NDMA
        self.dnext = 0
        self.waited = {k: {} for k in self.engs}
        self.ninst = 0
        self.nwaits = 0
        self.tag = ''
        self.log = {}

    def _sem(self, key):
        return self.sems[key] if isinstance(key, str) else self.dsems[key]

    def _wait(self, e, key, val):
        if key == e and e == "pe":
            return
        w = self.waited[e]
        if w.get(key, 0) >= val:
            return
        w[key] = val
        self.engs[e].wait_ge(self._sem(key), val)
        self.nwaits += 1

    def _deps(self, e, reads, writes):
        for b in reads:
            if b.w is not None:
                self._wait(e, *b.w)
        for b in writes:
            if b.w is not None:
                self._wait(e, *b.w)
            for k, v in b.r.items():
                self._wait(e, k, v)

    def _mark(self, tok, reads, writes):
        for b in reads:
            if b.r.get(tok[0], 0) < tok[1]:
                b.r[tok[0]] = tok[1]
        for b in writes:
            b.w = tok
            b.r = {}

    def op(self, e, fn, reads=(), writes=(), banks=()):
        writes = list(writes) + list(banks)
        self._deps(e, reads, writes)
        ins = fn(self.engs[e])
        self.cnt[e] += 1
        ins.then_inc(self.sems[e], 1)
        self._mark((e, self.cnt[e]), reads, writes)
        if self.tag is not None:
            self.log[(e, self.cnt[e])] = self.tag
        self.ninst += 1
        return ins

    def group(self, e, fns, reads=(), writes=(), banks=()):
        writes = list(writes) + list(banks)
        self._deps(e, reads, writes)
        ins = None
        for fn in fns:
            ins = fn(self.engs[e])
            self.ninst += 1
        self.cnt[e] += 1
        ins.then_inc(self.sems[e], 1)
        self._mark((e, self.cnt[e]), reads, writes)
        if self.tag is not None:
            self.log[(e, self.cnt[e])] = self.tag
        return ins

    def dma(self, out, in_, reads=(), writes=(), q="sp"):
        k = self.dnext
        self.dnext = (self.dnext + 1) % self.NDMA
        if self.dval[k] > 0:
            self._wait(q, k, self.dval[k])
        self._deps(q, reads, writes)
        ins = self.engs[q].dma_start(out=out, in_=in_)
        self.dval[k] += 16
        ins.then_inc(self.dsems[k], 16)
        self._mark((k, self.dval[k]), reads, writes)
        self.ninst += 1
        return ins

    def drain_dmas(self, q="sp"):
        for k in range(self.NDMA):
            if self.dval[k]:
                self._wait(q, k, self.dval[k])


def build(T, depth):
    NT = T + 1
    nc = bass.Bass("TRN2", target_bir_lowering=False)
    dt = lambda name, shape, kind="ExternalInput", dtype=F32: nc.dram_tensor(name, shape, dtype, kind=kind).ap()
    x_d = dt("x", [T * 128, D_MODEL])
    meta_d = dt("meta", [N_META, D_MODEL])
    win_d = dt("w_in", [depth, D_MODEL, IN_W])
    wout_d = dt("w_out", [depth, 2048, D_MODEL])
    ng_d = dt("ng_fm", [128, depth * 8])
    og_d = dt("og_fm", [128, depth * 8])
    gq_d = dt("gq_b", [128, depth * 64])
    gk_d = dt("gk_b", [128, depth * 64])
    sink_d = dt("sink_b", [128, depth * 16])
    convw_d = dt("convw_fm", [128, depth * 32])
    convb_d = dt("convb_row", [1, depth * 1024])
    bif_d = dt("bif_b", [128, depth * 8])
    rope_d = dt("rope", [NT, 128, 32])
    cst_d = dt("cst", [128, 640])
    y_d = dt("y", [T * 128, D_MODEL], kind="ExternalOutput")
    xs_d = dt("xs", [NT, 128, D_MODEL], kind="Internal")

    with ExitStack() as es:
        S = Sched(nc, es)
        sb = lambda name, shape, dtype=F32: es.enter_context(nc.sbuf_tensor(name, shape, dtype))
        ps = lambda name, shape, dtype=F32: es.enter_context(nc.psum_tensor(name, shape, dtype))

        win = sb("win", [128, 8, IN_W], BF16)
        wout = sb("wout", [128, 16, 1024], BF16)
        WINB = {(k, s): Buf(f"win{k}_{s}") for k in range(8) for s in range(len(SEGS))}
        WOUTB = {(k, g): Buf(f"wout{k}_{g}") for k in range(16) for g in range(2)}
        stage = [sb(f"stage{i}", [128, 512]) for i in range(2)]
        STAGE = [Buf(f"stage{i}") for i in range(2)]
        cst = sb("cst_sb", [128, 384]); CST = Buf("cst")
        ident_f, tri_f, ones_f = cst[:, 0:128], cst[:, 128:256], cst[:, 256:384]
        ident_bf = sb("ident_bf", [128, 128], BF16)
        mask01 = sb("mask01", [128, 128], BF16)
        mcur4 = sb("mcur4", [128, 4, 128], BF16)
        mprev4 = sb("mprev4", [128, 4, 128], BF16)
        onesrow = sb("onesrow", [1, 128], BF16)
        CONSTB = Buf("constb")
        ng = sb("ng", [128, depth * 8]); og = sb("og", [128, depth * 8])
        gq = sb("gq", [128, 64]); gk = sb("gk", [128, 64]); PARL = Buf("parl")
        sink = sb("sink", [128, 16]); esink = sb("esink", [128, 16]); ESINK = Buf("esink")
        convw = sb("convw", [128, depth * 32])
        convb_bf = sb("convb_bf", [1, 1024], BF16); CONVB = Buf("convb")
        bif = sb("bif", [128, depth * 8])
        PAR = Buf("params")
        diag = sb("diag", [128, 32, 128], BF16); DIAG = Buf("diag")
        neghalf = sb("neghalf", [128, 18]);
        xt = [sb(f"xt{i}", [128, 1024]) for i in range(2)]; XT = [Buf(f"xt{i}") for i in range(2)]
        ropet = [sb(f"ropet{i}", [128, 32]) for i in range(2)]; ROPE = [Buf(f"rope{i}") for i in range(2)]
        ss = sb("ss", [128, 1]); SS = Buf("ss")
        rstd = sb("rstd", [128, 1]); RSTD = Buf("rstd")

        hT = sb("hT", [128, 8, 128], BF16); HT = [Buf("hT0"), Buf("hT1")]
        junk = sb("junk", [128, 1024], BF16); JUNK = [Buf("junk0"), Buf("junk1")]
        ssq = sb("ssq", [128, 18]); SSQ = [Buf("ssq0"), Buf("ssq1")]
        rq = sb("rq", [128, 18]); RQ = [Buf("rq0"), Buf("rq1")]
        qb = sb("qb", [128, 18, 64], BF16); QB = [Buf("qb0"), Buf("qb1")]
        rtmp = stage[0][:, :].rearrange("p (a b) -> p a b", a=4); RTMP = STAGE[0]
        qT = sb("qT", [128, 16, 128], BF16); QT = [Buf(f"qT{i}") for i in range(4)]
        kvg = stage[1]; KVG = STAGE[1]
        SSK = Buf("ssk")
        RK = Buf("rk")
        kb = qb[:, 16:18, :]; KB = Buf("kb")
        kT = [sb(f"kT{i}", [128, 2, 128], BF16) for i in range(2)]; KT = [Buf(f"kT{i}") for i in range(2)]
        kTm = sb("kTm", [128, 2, 16], BF16); KTM = Buf("kTm")
        vaug = [sb(f"vaug{i}", [128, 2, 65], BF16) for i in range(2)]; VAUG = [Buf(f"vaug{i}") for i in range(2)]
        vm = sb("vm", [128, 2, 65], BF16); VM = Buf("vm")
        G = sb("G", [128, 1024], BF16); GB = [Buf("G0"), Buf("G1")]
        xn = G
        probs = sb("probs", [128, 3, 512], BF16); PROBS = [Buf(f"probs{i}") for i in range(3)]
        den = sb("den", [128, 4]); DEN = Buf("den")
        y = sb("y_sb", [128, 2048], BF16); YA = [Buf(f"ya{i}") for i in range(4)]; YM = [Buf(f"ym{i}") for i in range(4)]
        yT = sb("yT", [128, 16, 128], BF16); YT = [Buf(f"yT{i}") for i in range(4)]
        xqk = sb("xqk", [128, 8, 132], BF16); XQK = [Buf("xqk0"), Buf("xqk1")]

        qkm = sb("qkm", [128, 8, 128], BF16); QKM = [Buf("qkm0"), Buf("qkm1")]
        ktok = sb("ktok", [128, 4, 128], BF16); KTOK = Buf("ktok")
        gl = sb("gl", [128, 8]); GL = Buf("gl")
        gt = sb("gt", [128, 4]); GT = Buf("gt")
        dv = sb("dv", [128, 4]); DV = Buf("dv")
        wv = sb("wv", [128, 4]); WV = Buf("wv")
        uv = sb("uv", [128, 4]); UV = Buf("uv")
        ev = [sb(f"ev{i}", [128, 4]) for i in range(2)]; EV = [Buf(f"ev{i}") for i in range(2)]
        vp = sb("vp", [128, 4, 258], BF16); VP = [Buf(f"vp{i}") for i in range(4)]
        Sm = sb("Sm", [128, 4, 128], BF16); SM = [Buf(f"Sm{i}") for i in range(4)]
        thc = Sm
        Dst = sb("Dst", [128, 4, 257]); DST = [Buf(f"D{i}") for i in range(4)]
        Cbf = sb("Cbf", [128, 4, 257], BF16); CBF = [Buf(f"Cbf{i}") for i in range(4)]
        to = sb("to", [128, 1024], BF16); TO = [Buf("to0"), Buf("to1")]
        tgm = junk; TGM = JUNK
        hs = sb("hs", [128, 8, 4]); HS = [Buf(f"hs{i}") for i in range(4)]

        pz = [ps(f"pz{i}", [128, 512]) for i in range(3)]; PZ = [Buf(f"pz{i}") for i in range(3)]
        pt = [ps(f"pt{i}", [128, 1024], BF16) for i in range(2)]; PT = [Buf(f"pt{i}") for i in range(2)]
        pa = ps("pa", [128, 512]); PA = Buf("pa")
        pb = ps("pb", [128, 512]); PB = Buf("pb")
        pc = ps("pc", [128, 512]); PC = Buf("pc")
        zctr = [0]
        tctr = [0]

        def zbank():
            i = zctr[0] % 3
            zctr[0] += 1
            return pz[i], PZ[i]

        def tbank():
            i = tctr[0] % 2
            tctr[0] += 1
            return pt[i], PT[i]

        S.dma(cst[:], cst_d[:, 0:384], writes=[CST])
        S.dma(stage[0][:, 0:128], cst_d[:, 384:512], writes=[STAGE[0]])
        S.dma(stage[1][:, 0:128], cst_d[:, 512:640], writes=[STAGE[1]])
        for (t_sb, t_d) in [(ng, ng_d), (og, og_d), (convw, convw_d), (bif, bif_d)]:
            S.dma(t_sb[:], t_d[:, :], writes=[PAR])
        S.op("dve", lambda e: e.tensor_copy(out=ident_bf[:], in_=ident_f), reads=[CST], writes=[CONSTB])
        S.op("dve", lambda e: e.tensor_copy(out=mask01[:], in_=tri_f), reads=[CST], writes=[CONSTB])
        for i in range(4):
            S.op("dve", lambda e, i=i: e.tensor_copy(out=mcur4[:, i, :], in_=stage[0][:, 0:128]), reads=[STAGE[0]], writes=[CONSTB])
            S.op("dve", lambda e, i=i: e.tensor_copy(out=mprev4[:, i, :], in_=stage[1][:, 0:128]), reads=[STAGE[1]], writes=[CONSTB])
        S.op("dve", lambda e: e.tensor_copy(out=onesrow[:], in_=cst[0:1, 256:384]), reads=[CST], writes=[CONSTB])
        S.op("pool", lambda e: e.memset(neghalf[:], -0.5), writes=[CONSTB])
        for i in range(2):
            S.op("pool", lambda e, i=i: e.memset(vaug[i][:, :, 64:65], 1.0), writes=[VAUG[i]])
        S.op("pool", lambda e: e.memset(vm[:, :, 64:65], 1.0), writes=[VM])

        XT1H = [Buf('xt1a'), Buf('xt1b')]
        wst = [(stage[0], STAGE[0]), (stage[1], STAGE[1]), (xt[1][:, 0:512], XT1H[0]), (xt[1][:, 512:1024], XT1H[1])]

        def load_layer_weights(l):
            i = 0
            engs = ["dve", "act", "pool"]
            for k in range(8):
                for si, (d0, w, s0) in enumerate(SEGS):
                    st, STB = wst[i % 4]
                    S.dma(st[:, :w], win_d[l, k * 128:(k + 1) * 128, d0:d0 + w], writes=[STB] + ([XT[1]] if i in (2, 3) else []))
                    eng = engs[i % 3]
                    sc = ng[:, l * 8 + k:l * 8 + k + 1]
                    o_ap = win[:, k, s0:s0 + w]
                    if eng == "dve":
                        S.op("dve", lambda e, o_ap=o_ap, st=st, w=w, sc=sc: e.tensor_scalar(out=o_ap, in0=st[:, :w], scalar1=sc, scalar2=None, op0=ALU.mult),
                             reads=[STB, PAR], writes=[WINB[(k, si)]])
                    elif eng == "act":
                        S.op("act", lambda e, o_ap=o_ap, st=st, w=w, sc=sc: e.activation(out=o_ap, in_=st[:, :w], func=AF.Copy, scale=sc),
                             reads=[STB, PAR], writes=[WINB[(k, si)]])
                    else:
                        S.op("pool", lambda e, o_ap=o_ap, st=st, w=w, sc=sc: e.tensor_scalar(out=o_ap, in0=st[:, :w], scalar1=sc, scalar2=0.0, op0=ALU.mult, op1=ALU.add),
                             reads=[STB, PAR], writes=[WINB[(k, si)]])
                    i += 1
            for k in range(16):
                for g in range(2):
                    st, STB = wst[i % 4]
                    S.dma(st[:, :512], wout_d[l, k * 128:(k + 1) * 128, g * 512:(g + 1) * 512], writes=[STB])
                    eng = engs[i % 3]
                    o_ap = wout[:, k, g * 512:(g + 1) * 512]
                    if k < 8:
                        if eng == "act":
                            S.op("act", lambda e, o_ap=o_ap, st=st: e.copy(out=o_ap, in_=st[:, :512]), reads=[STB], writes=[WOUTB[(k, g)]])
                        else:
                            S.op(eng, lambda e, o_ap=o_ap, st=st: e.tensor_copy(out=o_ap, in_=st[:, :512]), reads=[STB], writes=[WOUTB[(k, g)]])
                    else:
                        sc = og[:, l * 8 + k - 8:l * 8 + k - 7]
                        if eng == "dve":
                            S.op("dve", lambda e, o_ap=o_ap, st=st, sc=sc: e.tensor_scalar(out=o_ap, in0=st[:, :512], scalar1=sc, scalar2=None, op0=ALU.mult),
                                 reads=[STB, PAR], writes=[WOUTB[(k, g)]])
                        elif eng == "act":
                            S.op("act", lambda e, o_ap=o_ap, st=st, sc=sc: e.activation(out=o_ap, in_=st[:, :512], func=AF.Copy, scale=sc),
                                 reads=[STB, PAR], writes=[WOUTB[(k, g)]])
                        else:
                            S.op("pool", lambda e, o_ap=o_ap, st=st, sc=sc: e.tensor_scalar(out=o_ap, in0=st[:, :512], scalar1=sc, scalar2=0.0, op0=ALU.mult, op1=ALU.add),
                                 reads=[STB, PAR], writes=[WOUTB[(k, g)]])
                    i += 1
            for c in range(8):
                for tap in range(4):
                    sc = convw[:, l * 32 + tap * 8 + c:l * 32 + tap * 8 + c + 1]
                    S.op("pool", lambda e, c=c, tap=tap, sc=sc: e.tensor_scalar(out=diag[:, c * 4 + tap, :], in0=ident_f, scalar1=sc, scalar2=0.0, op0=ALU.mult, op1=ALU.add),
                         reads=[CST, PAR], writes=[DIAG])
            S.dma(gq[:], gq_d[:, l * 64:(l + 1) * 64], writes=[PARL])
            S.dma(gk[:], gk_d[:, l * 64:(l + 1) * 64], writes=[PARL])
            S.dma(sink[:], sink_d[:, l * 16:(l + 1) * 16], writes=[PARL])
            S.dma(xt[1][0:1, :], convb_d[0:1, l * 1024:(l + 1) * 1024], writes=[XT[1], XT1H[0], XT1H[1]])
            S.op("dve", lambda e: e.tensor_copy(out=convb_bf[:], in_=xt[1][0:1, :]), reads=[XT[1]], writes=[CONVB])
            S.op("act", lambda e: e.activation(out=esink[:], in_=sink[:, :], func=AF.Exp, bias=math.log(2.0)), reads=[PARL], writes=[ESINK])
            S.op("pool", lambda e: e.memset(xqk[:, :, 0:3], 0.0), writes=XQK)

        def load_x(l, j):
            par = j % 2
            n = 16 if j == 0 else 128
            if l == 0:
                if j == 0:
                    S.dma(xt[par][:16, :], meta_d[:, :], writes=[XT[par]])
                else:
                    S.dma(xt[par][:, :], x_d[(j - 1) * 128:j * 128, :], writes=[XT[par]])
            else:
                S.dma(xt[par][:n, :], xs_d[j, :n, :], reads=[XS[j]], writes=[XT[par]])
            S.dma(ropet[par][:n, :], rope_d[j, :n, :], writes=[ROPE[par]])

        XS = [Buf(f"xs{j}") for j in range(NT)]

        def do_rope_all(n, par):
            nh = 18
            x16 = qb[:n, :, 0:16]
            x1 = qb[:n, :, 0:8]
            x2 = qb[:n, :, 8:16]
            ta = stage[0][:n, 0:288].rearrange("p (h d) -> p h d", d=16)
            tb_ = stage[0][:n, 288:432].rearrange("p (h d) -> p h d", d=8)
            tc_ = stage[1][:n, 264:408].rearrange("p (h d) -> p h d", d=8)
            cc = ropet[par][:n, 0:16].unsqueeze(1).broadcast_to([n, nh, 16])
            ms = ropet[par][:n, 16:24].unsqueeze(1).broadcast_to([n, nh, 8])
            ps_ = ropet[par][:n, 24:32].unsqueeze(1).broadcast_to([n, nh, 8])
            R = [ROPE[par], QB[0], QB[1], KB]
            S.op("pool", lambda e: e.tensor_tensor(out=ta, in0=x16, in1=cc, op=ALU.mult), reads=R, writes=[RTMP])
            S.op("pool", lambda e: e.tensor_tensor(out=tb_, in0=x2, in1=ms, op=ALU.mult), reads=R, writes=[RTMP])
            S.op("pool", lambda e: e.tensor_tensor(out=tc_, in0=x1, in1=ps_, op=ALU.mult), reads=R, writes=[KVG])
            S.op("pool", lambda e: e.tensor_tensor(out=x1, in0=ta[:, :, 0:8], in1=tb_, op=ALU.add), reads=[RTMP], writes=[QB[0], QB[1], KB])
            S.op("pool", lambda e: e.tensor_tensor(out=x2, in0=ta[:, :, 8:16], in1=tc_, op=ALU.add), reads=[RTMP, KVG], writes=[QB[0], QB[1], KB])

        def zgroup(n, name, c0, w, evac, bank=None):
            S.tag = S.tag.split('/')[0] + '/' + name
            zb, ZB = zbank() if bank is None else bank
            rd = [HT[0], HT[1]] + [WINB[(k, s)] for k in range(8) for s in SEG_OF[name]]
            S.group("pe", [lambda e, k=k: e.matmul(out=zb[:n, :w], lhsT=hT[:, k, :n], rhs=win[:, k, c0:c0 + w], start=(k == 0), stop=(k == 7))
                           for k in range(8)], reads=rd, banks=[ZB])
            evac(zb, ZB)

        def pre_fn(l, j):
            n = 16 if j == 0 else 128
            par = j % 2
            xt_, X = xt[par], XT[par]
            S.op("act", lambda e: e.activation(out=junk[:n, :], in_=xt_[:n, :], func=AF.Square, accum_out=ss[:n, :]), reads=[X], writes=[JUNK[0], JUNK[1], SS])
            S.op("pool", lambda e: e.tensor_scalar(out=rstd[:n, :], in0=ss[:n, :], scalar1=1.0 / D_MODEL, scalar2=EPS, op0=ALU.mult, op1=ALU.add), reads=[SS], writes=[RSTD])
            S.op("pool", lambda e: e.tensor_tensor(out=rstd[:n, :], in0=rstd[:n, :], in1=neghalf[:n, 0:1], op=ALU.pow), reads=[RSTD, CONSTB], writes=[RSTD])
            S.op("dve", lambda e: e.tensor_scalar(out=xn[:n, :], in0=xt_[:n, :], scalar1=rstd[:n, 0:1], scalar2=None, op0=ALU.mult), reads=[X, RSTD], writes=GB)

        def ht_fn(l, j):
            n = 16 if j == 0 else 128
            for hb in range(2):
                tb, TB = tbank()
                S.group("pe", [lambda e, k=k, tb=tb: e.transpose(out=tb[:, (k % 4) * 128:(k % 4) * 128 + n], in_=xn[:n, k * 128:(k + 1) * 128], identity=ident_bf[:n, :n])
                               for k in range(4 * hb, 4 * hb + 4)], reads=GB + [CONSTB], banks=[TB])
                src = tb[:, 0:512].rearrange("p (c t) -> p c t", c=4)[:, :, :n]
                if hb == 0:
                    S.op("act", lambda e, src=src: e.copy(out=hT[:, 0:4, :n], in_=src), banks=[TB], writes=[HT[0]])
                else:
                    S.op("dve", lambda e, src=src: e.tensor_copy(out=hT[:, 4:8, :n], in_=src), banks=[TB], writes=[HT[1]])

        def tile_fn(l, j, last_layer, nxt):
            n = 16 if j == 0 else 128
            par = j % 2
            xt_, X = xt[par], XT[par]
            if j + 1 < NT:
                load_x(l, j + 1)
            vdst, VD = (vm, VM) if j == 0 else (vaug[par], VAUG[par])
            ktd, KTD = (kTm, KTM) if j == 0 else (kT[par], KT[par])

            S.tag = f'L{l}T{j}:z'
            def mk_ev_ag(gi):
                def ev(zb, ZB):
                    S.op("act", lambda e: e.activation(out=G[:n, gi * 512:(gi + 1) * 512], in_=zb[:n, :], func=AF.Tanh, scale=0.5), banks=[ZB], writes=[GB[gi]])
                    S.op("dve", lambda e: e.scalar_tensor_tensor(out=G[:n, gi * 512:(gi + 1) * 512], in0=G[:n, gi * 512:(gi + 1) * 512], scalar=1.0, in1=zb[:n, :], op0=ALU.add, op1=ALU.mult), banks=[ZB], writes=[GB[gi]])
                return ev

            def after_kv2b():
                S.op("act", lambda e: e.activation(out=uv[:n, :], in_=dv[:n, :], func=AF.Exp, bias=LN_CK), reads=[DV], writes=[UV])

            def after_kv2():
                S.group("pe", [lambda e: e.matmul(out=pa[:n, 0:4], lhsT=tri_f[:n, :n], rhs=gt[:n, :], start=True, stop=True),
                               lambda e: e.matmul(out=pa[:, 8:12], lhsT=ones_f[:n, :], rhs=gt[:n, :], start=True, stop=True)],
                        reads=[CST, GT], banks=[PA])
                S.op("act", lambda e: e.activation(out=wv[:n, :], in_=pa[:n, 0:4], func=AF.Exp, scale=-1.0, bias=LN_HALF), banks=[PA], writes=[WV])
                S.op("act", lambda e: e.activation(out=ev[par][:, :], in_=pa[:, 8:12], func=AF.Exp, scale=-1.0), banks=[PA], writes=[EV[par]])
                S.op("dve", lambda e: e.tensor_tensor(out=dv[:n, :], in0=pa[:n, 0:4], in1=gl[:n, 0:4], op=ALU.add), banks=[PA], reads=[GL], writes=[DV])

            def mk_ev_mv(gi):
                def ev(zb, ZB):
                    for hh in range(2):
                        h = 2 * gi + hh
                        S.op("act", lambda e, h=h, hh=hh: e.activation(out=vp[:n, h, 0:256], in_=zb[:n, hh * 256:(hh + 1) * 256], func=AF.Copy, scale=uv[:n, h:h + 1]), banks=[ZB], reads=[UV], writes=[VP[h]])
                        S.op("dve", lambda e, h=h: e.tensor_copy(out=vp[:n, h, 256:257], in_=uv[:n, h:h + 1]), reads=[UV], writes=[VP[h]])
                return ev

            def mk_ev_mo(gi):
                def ev(zb, ZB):
                    S.op("act", lambda e: e.activation(out=to[:n, gi * 512:(gi + 1) * 512], in_=zb[:n, :], func=AF.Tanh, scale=0.5), banks=[ZB], writes=[TO[gi]])
                return ev

            def mk_ev_mg(gi):
                def ev(zb, ZB):
                    sl = slice(gi * 512, (gi + 1) * 512)
                    S.op("act", lambda e: e.activation(out=tgm[:n, sl], in_=zb[:n, :], func=AF.Tanh, scale=0.5), banks=[ZB], writes=[TGM[gi]])
                    S.op("dve", lambda e: e.scalar_tensor_tensor(out=tgm[:n, sl], in0=tgm[:n, sl], scalar=1.0, in1=zb[:n, :], op0=ALU.add, op1=ALU.mult), banks=[ZB], writes=[TGM[gi]])
                    S.op("dve", lambda e: e.scalar_tensor_tensor(out=to[:n, sl], in0=to[:n, sl], scalar=1.0, in1=tgm[:n, sl], op0=ALU.add, op1=ALU.mult), reads=[TGM[gi]], writes=[TO[gi]])
                return ev

            def fm_z(half):
                seg = "mq" if half == 0 else "mk"
                c0 = 2312 + half * 512

                def ev(zb, ZB):
                    if half == 0:
                        S.op("act", lambda e: e.copy(out=thc[:n, :, :], in_=zb[:n, :].rearrange("p (c d) -> p c d", c=4)), banks=[ZB], writes=SM)
                    else:
                        S.op("act", lambda e: e.copy(out=qkm[:n, 4:8, :], in_=zb[:n, :].rearrange("p (c d) -> p c d", c=4)), banks=[ZB], writes=[QKM[1]])
                zgroup(n, seg, c0, 512, ev)

            def fm_T(half):
                tb, TB = tbank()
                srcs = thc if half == 0 else qkm
                off = 0 if half == 0 else 4
                S.group("pe", [lambda e, cc=cc, tb=tb: e.transpose(out=tb[:, cc * 128:cc * 128 + n], in_=srcs[:n, off + cc, :], identity=ident_bf[:n, :n]) for cc in range(4)],
                        reads=(SM if half == 0 else [QKM[1]]) + [CONSTB], banks=[TB])
                src = tb[:, 0:512].rearrange("p (c t) -> p c t", c=4)[:, :, :n]
                if half == 0:
                    S.op("act", lambda e, src=src: e.copy(out=xqk[:, 0:4, 3:3 + n], in_=src), banks=[TB], writes=[XQK[0]])
                else:
                    S.op("act", lambda e, src=src: e.copy(out=xqk[:, 4:8, 3:3 + n], in_=src), banks=[TB], writes=[XQK[1]])

            def conv_half(half):
                pcv, PCV = zbank()
                fns = []
                for cc in range(4):
                    c = half * 4 + cc
                    for tap in range(4):
                        fns.append(lambda e, cc=cc, c=c, tap=tap: e.matmul(out=pcv[:, cc * 128:cc * 128 + n], lhsT=diag[:, c * 4 + tap, :], rhs=xqk[:, c, tap:tap + n], start=(tap == 0), stop=False))
                    fns.append(lambda e, cc=cc, c=c: e.matmul(out=pcv[:, cc * 128:cc * 128 + n], lhsT=convb_bf[0:1, c * 128:(c + 1) * 128], rhs=onesrow[0:1, :n], start=False, stop=True))
                S.group("pe", fns, reads=[DIAG, XQK[half], CONVB, CONSTB], banks=[PCV])
                src = pcv[:, :].rearrange("p (c t) -> p c t", c=4)[:, :, :n]
                S.op("act", lambda e, src=src: e.activation(out=thc[:, :, :n], in_=src, func=AF.Tanh, scale=0.5), banks=[PCV], writes=SM)
                S.op("dve", lambda e, src=src, half=half: e.scalar_tensor_tensor(out=qkm[:, 4 * half:4 * half + 4, :n], in0=thc[:, :, :n], scalar=1.0, in1=src, op0=ALU.add, op1=ALU.mult), banks=[PCV], reads=SM, writes=[QKM[half]])
                S.op("pool", lambda e, half=half: e.tensor_copy(out=xqk[:, 4 * half:4 * half + 4, 0:3], in_=xqk[:, 4 * half:4 * half + 4, n:n + 3]), writes=[XQK[half]])

            def kT_fn():
                tb, TB = tbank()
                S.group("pe", [lambda e, g=g, tb=tb: e.transpose(out=tb[:64, g * 128:g * 128 + n], in_=kb[:n, g, :], identity=ident_bf[:n, :n]) for g in range(2)],
                        reads=[KB, CONSTB], banks=[TB])
                src = tb[:64, 0:256].rearrange("p (g t) -> p g t", g=2)[:, :, :n]
                S.op("act", lambda e, src=src: e.copy(out=ktd[:64, :, :n], in_=src), banks=[TB], writes=[KTD])

            def qT_fn(qq):
                tb, TB = tbank()
                S.group("pe", [lambda e, i=i, tb=tb: e.transpose(out=tb[:64, i * 128:i * 128 + n], in_=qb[:n, qq * 4 + i, :], identity=ident_bf[:n, :n]) for i in range(4)],
                        reads=[QB[0], QB[1], CONSTB], banks=[TB])
                src = tb[:64, 0:512].rearrange("p (g t) -> p g t", g=4)[:, :, :n]
                if qq % 2 == 0:
                    S.op("act", lambda e, src=src, qq=qq: e.copy(out=qT[:64, qq * 4:qq * 4 + 4, :n], in_=src), banks=[TB], writes=[QT[qq]])
                else:
                    S.op("dve", lambda e, src=src, qq=qq: e.tensor_copy(out=qT[:64, qq * 4:qq * 4 + 4, :n], in_=src), banks=[TB], writes=[QT[qq]])

            def attn_groups(u):
                g = u // 2
                groups = []
                groups.append((pa, PA, n, ktd[:64, g, :n], mcur4, vdst, VD, KTD))
                if j >= 2:
                    groups.append((pb, PB, 128, kT[1 - par][:64, g, :], mprev4, vaug[1 - par], VAUG[1 - par], KT[1 - par]))
                if j >= 1:
                    groups.append((pc, PC, 16, kTm[:64, g, :], None, vm, VM, KTM))
                return groups

            def attn_s(u):
                rq_ap = qT[:64, u * 4:u * 4 + 4, :n]
                for gi_, (pp, PP, nk, kl, msk, vt, VB, KBF) in enumerate(attn_groups(u)):
                    out_ap = pp[:nk, 0:4 * n].rearrange("p (h t) -> p h t", h=4)
                    fns = [lambda e, out_ap=out_ap, kl=kl, msk=msk: e.matmul(out=out_ap, lhsT=kl, rhs=rq_ap, start=True, stop=(msk is None))]
                    if msk is not None:
                        fns.append(lambda e, out_ap=out_ap, msk=msk, nk=nk: e.matmul(out=out_ap, lhsT=ident_bf[:nk, :nk], rhs=msk[:nk, :, :n], start=False, stop=True))
                    S.group("pe", fns, reads=[QT[u], KBF, CONSTB], banks=[PP])
                    S.op("act", lambda e, pp=pp, nk=nk, gi_=gi_: e.activation(out=probs[:nk, gi_, 0:4 * n], in_=pp[:nk, 0:4 * n], func=AF.Exp, scale=0.125), banks=[PP], writes=[PROBS[gi_]])

            def attn_pv(u):
                g = u // 2
                groups = attn_groups(u)
                zb, ZB = zbank()
                fns = []
                for i in range(4):
                    for gi_, (pp, PP, nk, kl, msk, vt, VB, KBF) in enumerate(groups):
                        fns.append(lambda e, i=i, gi_=gi_, nk=nk, vt=vt: e.matmul(out=zb[:n, i * 65:(i + 1) * 65], lhsT=probs[:nk, gi_, i * n:(i + 1) * n], rhs=vt[:nk, g, :],
                                                                              start=(gi_ == 0), stop=(gi_ == len(groups) - 1)))
                S.group("pe", fns, reads=[PROBS[i] for i in range(len(groups))] + [gr[6] for gr in groups], banks=[ZB])
                o3 = zb[:n, 0:260].rearrange("p (h d) -> p h d", h=4)
                S.op("dve", lambda e, o3=o3: e.scalar_tensor_tensor(out=den[:n, :], in0=o3[:, :, 64], scalar=2.0, in1=esink[:n, u * 4:u * 4 + 4], op0=ALU.mult, op1=ALU.add), banks=[ZB], reads=[ESINK], writes=[DEN])
                S.op("dve", lambda e: e.reciprocal(out=den[:n, :], in_=den[:n, :]), reads=[DEN], writes=[DEN])
                for i in range(4):
                    hd = u * 4 + i
                    S.op("dve", lambda e, i=i, hd=hd, o3=o3: e.scalar_tensor_tensor(out=y[:n, hd * 64:(hd + 1) * 64], in0=o3[:, i, 0:64], scalar=den[:n, i:i + 1], in1=G[:n, hd * 64:(hd + 1) * 64], op0=ALU.mult, op1=ALU.mult),
                         banks=[ZB], reads=[DEN, GB[hd // 8]], writes=[YA[u]])

            pbanks = [(pb, PB), (pc, PC), None, None]
            cbanks = [None, (pa, PA)]

            def ml_a():
                tb, TB = tbank()
                S.group("pe", [lambda e, h=h, tb=tb: e.transpose(out=tb[:n, h * 128:(h + 1) * 128], in_=qkm[:, 4 + h, :n], identity=ident_bf[:, :]) for h in range(4)],
                        reads=[QKM[1], CONSTB], banks=[TB])
                S.op("act", lambda e, tb=tb: e.copy(out=ktok[:n, :, :], in_=tb[:n, 0:512].rearrange("p (h d) -> p h d", h=4)), banks=[TB], writes=[KTOK])
                S.group("pe", [lambda e, h=h: e.matmul(out=pa[:n, h * 128:h * 128 + n], lhsT=qkm[:, 4 + h, :n], rhs=qkm[:, h, :n], start=True, stop=True) for h in range(4)],
                        reads=[QKM[0], QKM[1]], banks=[PA])
                S.op("dve", lambda e: e.tensor_tensor(out=Sm[:n, :, :n], in0=pa[:n, :].rearrange("p (h t) -> p h t", h=4)[:, :, :n], in1=mask01[:n, :n].unsqueeze(1).broadcast_to([n, 4, n]), op=ALU.mult),
                     banks=[PA], reads=[CONSTB], writes=SM)

            def ml_P():
                pbanks[2] = zbank()
                pbanks[3] = zbank()
                for h in range(4):
                    PP_, PPB = pbanks[h]
                    fns = [lambda e, h=h, PP_=PP_: e.matmul(out=PP_[:n, 0:257], lhsT=Sm[:n, h, :n], rhs=vp[:n, h, 0:257], start=True, stop=(j == 0))]
                    if j > 0:
                        fns.append(lambda e, h=h, PP_=PP_: e.matmul(out=PP_[:n, 0:257], lhsT=qkm[:, h, :n], rhs=Cbf[:, h, :], start=False, stop=True))
                    S.group("pe", fns, reads=[SM[h], VP[h], QKM[0], CBF[h]], banks=[PPB])

            def ml_C():
                cb = zbank()
                while cb[1] in (pbanks[2][1], pbanks[3][1]):
                    cb = zbank()
                cbanks[0] = cb
                for h in range(4):
                    zb, ZB = cbanks[h % 2]
                    S.group("pe", [lambda e, h=h, zb=zb: e.matmul(out=zb[:, 0:257], lhsT=ktok[:n, h, :], rhs=vp[:n, h, 0:257], start=True, stop=True)], reads=[KTOK, VP[h]], banks=[ZB])
                    if j == 0:
                        S.op("dve", lambda e, h=h, zb=zb: e.tensor_copy(out=Dst[:, h, :], in_=zb[:, 0:257]), banks=[ZB], writes=[DST[h]])
                    else:
                        S.op("dve", lambda e, h=h, zb=zb: e.scalar_tensor_tensor(out=Dst[:, h, :], in0=Dst[:, h, :], scalar=ev[1 - par][:, h:h + 1], in1=zb[:, 0:257], op0=ALU.mult, op1=ALU.add),
                             banks=[ZB], reads=[EV[1 - par]], writes=[DST[h]])
                    S.op("pool", lambda e, h=h: e.tensor_scalar(out=Cbf[:, h, :], in0=Dst[:, h, :], scalar1=ev[par][:, h:h + 1], scalar2=0.0, op0=ALU.mult, op1=ALU.add), reads=[DST[h], EV[par]], writes=[CBF[h]])

            def emit_chains():
                HSB = HS[0]
                for h in range(4):
                    PP_, PPB = pbanks[h]
                    S.op("dve", lambda e, PP_=PP_, h=h: e.tensor_tensor(out=hs[:n, 0, h:h + 1], in0=PP_[:n, 256:257], in1=wv[:n, h:h + 1], op=ALU.mult), banks=[PPB], reads=[WV], writes=[HSB])
                S.op("dve", lambda e: e.scalar_tensor_tensor(out=hs[:n, 1, :], in0=hs[:n, 0, :], scalar=-1.0, in1=hs[:n, 0, :], op0=ALU.mult, op1=ALU.max), reads=[HSB], writes=[HSB])
                S.op("dve", lambda e: e.tensor_scalar(out=hs[:n, 1, :], in0=hs[:n, 1, :], scalar1=1.0, scalar2=None, op0=ALU.max), reads=[HSB], writes=[HSB])
                S.op("dve", lambda e: e.reciprocal(out=hs[:n, 2, :], in_=hs[:n, 1, :]), reads=[HSB], writes=[HSB])
                S.op("dve", lambda e: e.tensor_tensor(out=hs[:n, 3, :], in0=hs[:n, 2, :], in1=wv[:n, :], op=ALU.mult), reads=[HSB, WV], writes=[HSB])
                for h in range(4):
                    PP_, PPB = pbanks[h]
                    S.op("act", lambda e, PP_=PP_, h=h: e.activation(out=junk[:n, h * 256:(h + 1) * 256], in_=PP_[:n, 0:256], func=AF.Square, scale=hs[:n, 3, h:h + 1], accum_out=hs[:n, 4, h:h + 1]),
                         banks=[PPB], reads=[HSB], writes=[JUNK[h // 2], HS[1]])
                S.op("pool", lambda e: e.tensor_scalar(out=hs[:n, 5, :], in0=hs[:n, 4, :], scalar1=1.0 / 256, scalar2=EPS, op0=ALU.mult, op1=ALU.add), reads=[HS[1]], writes=[HS[2]])
                S.op("pool", lambda e: e.tensor_tensor(out=hs[:n, 5, :], in0=hs[:n, 5, :], in1=neghalf[:n, 0:4], op=ALU.pow), reads=[HS[2], CONSTB], writes=[HS[2]])
                S.op("dve", lambda e: e.scalar_tensor_tensor(out=hs[:n, 6, :], in0=hs[:n, 5, :], scalar=0.25, in1=hs[:n, 3, :], op0=ALU.mult, op1=ALU.mult), reads=[HS[2], HSB], writes=[HS[3]])
                for h in range(4):
                    PP_, PPB = pbanks[h]
                    S.op("dve", lambda e, PP_=PP_, h=h: e.scalar_tensor_tensor(out=y[:n, 1024 + h * 256:1024 + (h + 1) * 256], in0=PP_[:n, 0:256], scalar=hs[:n, 6, h:h + 1], in1=to[:n, h * 256:(h + 1) * 256], op0=ALU.mult, op1=ALU.mult),
                         banks=[PPB], reads=[HS[3], TO[h // 2]], writes=[YM[h]])


            def front_staged():
                zbk, ZBK = zbank()
                rdk = [HT[0], HT[1]] + [WINB[(k, s_)] for k in range(8) for s_ in SEG_OF["kv"]]
                S.group("pe", [lambda e, k=k: e.matmul(out=zbk[:n, :264], lhsT=hT[:, k, :n], rhs=win[:, k, 1024:1288], start=(k == 0), stop=(k == 7)) for k in range(8)], reads=rdk, banks=[ZBK])
                S.op("dve", lambda e: e.tensor_copy(out=kvg[:n, :264], in_=zbk[:n, :264]), banks=[ZBK], writes=[KVG])
                qbk = [(pb, PB), (pc, PC)]
                for qi in range(2):
                    zb, ZB = qbk[qi]
                    rd = [HT[0], HT[1]] + [WINB[(k, s_)] for k in range(8) for s_ in SEG_OF["q%d" % qi]]
                    S.group("pe", [lambda e, k=k, zb=zb, qi=qi: e.matmul(out=zb[:n, :], lhsT=hT[:, k, :n], rhs=win[:, k, qi * 512:(qi + 1) * 512], start=(k == 0), stop=(k == 7)) for k in range(8)], reads=rd, banks=[ZB])
                    S.op("act", lambda e, zb=zb, qi=qi: e.activation(out=junk[:n, qi * 512:(qi + 1) * 512], in_=zb[:n, :], func=AF.Square), banks=[ZB], writes=[JUNK[qi]])
                S.op("act", lambda e: e.activation(out=ktok[:n, 0, :], in_=kvg[:n, 0:128], func=AF.Square), reads=[KVG], writes=[KTOK])
                S.op("dve", lambda e: e.tensor_tensor(out=gl[:n, :], in0=kvg[:n, 256:264], in1=bif[:n, l * 8:(l + 1) * 8], op=ALU.add), reads=[KVG, PAR], writes=[GL])
                for qi in range(2):
                    S.op("dve", lambda e, qi=qi: e.tensor_reduce(out=ssq[:n, 8 * qi:8 * qi + 8], in_=junk[:n, qi * 512:(qi + 1) * 512].rearrange("p (h d) -> p h d", h=8), axis=AX.X, op=ALU.add),
                         reads=[JUNK[qi]], writes=[SSQ[qi]])
                S.op("dve", lambda e: e.tensor_reduce(out=ssq[:n, 16:18], in_=ktok[:n, 0, :].rearrange("p (g d) -> p g d", g=2), axis=AX.X, op=ALU.add), reads=[KTOK], writes=[SSK])
                S.op("pool", lambda e: e.tensor_scalar(out=rq[:n, :], in0=ssq[:n, :], scalar1=1.0 / 64, scalar2=EPS, op0=ALU.mult, op1=ALU.add), reads=[SSQ[0], SSQ[1], SSK], writes=[RQ[0], RQ[1], RK])
                S.op("pool", lambda e: e.tensor_tensor(out=rq[:n, :], in0=rq[:n, :], in1=neghalf[:n, 0:18], op=ALU.pow), reads=[RQ[0], CONSTB], writes=[RQ[0], RQ[1], RK])
                S.op("act", lambda e: e.activation(out=gt[:n, :], in_=gl[:n, 4:8], func=AF.Exp, scale=-1.0), reads=[GL], writes=[GT])
                S.op("act", lambda e: e.activation(out=gt[:n, :], in_=gt[:n, :], func=AF.Ln, bias=1.0), reads=[GT], writes=[GT])
                S.op("act", lambda e: e.copy(out=vdst[:n, :, 0:64], in_=kvg[:n, 128:256].rearrange("p (g d) -> p g d", g=2)), reads=[KVG], writes=[VD])
                for qi in range(2):
                    zb, ZB = qbk[qi]
                    for h in range(8):
                        hh = 8 * qi + h
                        S.op("dve", lambda e, h=h, hh=hh, zb=zb: e.scalar_tensor_tensor(out=qb[:n, hh, :], in0=zb[:n, h * 64:(h + 1) * 64], scalar=rq[:n, hh:hh + 1], in1=gq[:n, :], op0=ALU.mult, op1=ALU.mult),
                             banks=[ZB], reads=[RQ[qi], PARL], writes=[QB[qi]])
                for g in range(2):
                    S.op("dve", lambda e, g=g: e.scalar_tensor_tensor(out=kb[:n, g, :], in0=kvg[:n, g * 64:(g + 1) * 64], scalar=rq[:n, 16 + g:17 + g], in1=gk[:n, :], op0=ALU.mult, op1=ALU.mult),
                         reads=[KVG, RK, PARL], writes=[KB])
                do_rope_all(n, par)

            T_ = f'L{l}T{j}:'
            seq = [('front', front_staged), ('z', lambda: fm_z(0)), ('z', lambda: fm_z(1)),
                   ('fmT0', lambda: fm_T(0)), ('z', lambda: zgroup(n, "mo0", 4360, 512, mk_ev_mo(0))),
                   ('fmT1', lambda: fm_T(1)), ('kv2', after_kv2), ('conv0', lambda: conv_half(0)),
                   ('z', lambda: zgroup(n, "ag0", 1288, 512, mk_ev_ag(0))),
                   ('conv1', lambda: conv_half(1)), ('kv2b', after_kv2b),
                   ('qkT', kT_fn), ('qkT', lambda: qT_fn(0)),
                   ('z', lambda: zgroup(n, "mv0", 3336, 512, mk_ev_mv(0))),
                   ('qkT', lambda: qT_fn(1)),
                   ('z', lambda: zgroup(n, "mo1", 4872, 512, mk_ev_mo(1))),
                   ('qkT', lambda: qT_fn(2)),
                   ('z', lambda: zgroup(n, "ag1", 1800, 512, mk_ev_ag(1))),
                   ('qkT', lambda: qT_fn(3)),
                   ('attn0', lambda: attn_s(0)), ('z', lambda: zgroup(n, "mv1", 3848, 512, mk_ev_mv(1))), ('attn0', lambda: attn_pv(0)),
                   ('attn1', lambda: attn_s(1)), ('z', lambda: zgroup(n, "mg0", 5384, 512, mk_ev_mg(0))), ('attn1', lambda: attn_pv(1)),
                   ('attn2', lambda: attn_s(2)), ('z', lambda: zgroup(n, "mg1", 5896, 512, mk_ev_mg(1))), ('attn2', lambda: attn_pv(2)),
                   ('attn3', lambda: attn_s(3)), ('ml_a', ml_a), ('attn3', lambda: attn_pv(3)),
                   ('pre', (lambda: pre_fn(*nxt)) if nxt is not None else (lambda: None)),
                   ('ml_P', ml_P), ('ml_C', ml_C)]
            for tg, fn in seq:
                S.tag = T_ + tg
                fn()

            S.tag = f'L{l}T{j}:pre'
            S.tag = f'L{l}T{j}:out'
            obanks = [(pa, PA), cbanks[0]]
            for part in range(2):
                if part == 1:
                    S.tag = f'L{l}T{j}:chains'
                    emit_chains()
                    if nxt is not None:
                        S.tag = f'L{l}T{j}:ht'
                        ht_fn(*nxt)
                    S.tag = f'L{l}T{j}:out1'
                for q4 in (2 * part, 2 * part + 1):
                    tb, TB = tbank()
                    S.group("pe", [lambda e, i=i, tb=tb: e.transpose(out=tb[:, i * 128:i * 128 + n], in_=y[:n, (q4 * 4 + i) * 128:(q4 * 4 + i + 1) * 128], identity=ident_bf[:n, :n]) for i in range(4)],
                            reads=(YA if part == 0 else YM) + [CONSTB], banks=[TB])
                    src = tb[:, 0:512].rearrange("p (c t) -> p c t", c=4)[:, :, :n]
                    if q4 % 2 == 0:
                        S.op("act", lambda e, src=src, q4=q4: e.copy(out=yT[:, q4 * 4:q4 * 4 + 4, :n], in_=src), banks=[TB], writes=[YT[q4]])
                    else:
                        S.op("dve", lambda e, src=src, q4=q4: e.tensor_copy(out=yT[:, q4 * 4:q4 * 4 + 4, :n], in_=src), banks=[TB], writes=[YT[q4]])
                for g in range(2):
                    zb, ZB = obanks[g]
                    S.group("pe", [lambda e, k=k, zb=zb: e.matmul(out=zb[:n, :], lhsT=yT[:, k, :n], rhs=wout[:, k, g * 512:(g + 1) * 512], start=(k == 0), stop=(k == 15)) for k in range(8 * part, 8 * part + 8)],
                            reads=YT[2 * part:2 * part + 2] + [WOUTB[(k, g)] for k in range(8 * part, 8 * part + 8)], banks=[ZB])
            for g in range(2):
                zb, ZB = obanks[g]
                S.op("dve", lambda e, zb=zb, g=g: e.tensor_tensor(out=xt_[:n, g * 512:(g + 1) * 512], in0=zb[:n, :], in1=xt_[:n, g * 512:(g + 1) * 512], op=ALU.add), banks=[ZB], writes=[X])
            if last_layer:
                if j > 0:
                    S.dma(y_d[(j - 1) * 128:j * 128, :], xt_[:, :], reads=[X], writes=[YOUT])
            else:
                S.dma(xs_d[j, :n, :], xt_[:n, :], reads=[X], writes=[XS[j]])

        YOUT = Buf("yout")
        for l in range(depth):
            load_x(l, 0)
            load_layer_weights(l)
            pre_fn(l, 0)
            ht_fn(l, 0)
            for j in range(NT):
                tile_fn(l, j, l == depth - 1, (l, j + 1) if j + 1 < NT else None)
        S.drain_dmas("sp")
        print("ninst", S.ninst, "nwaits", S.nwaits)
        build.last_log = S.log
    return nc


def host_consts():
    idx = np.arange(128)
    ident = np.eye(128, dtype=np.float32)
    tri = (idx[:, None] <= idx[None, :]).astype(np.float32)
    ones = np.ones((128, 128), np.float32)
    mcur = np.where(idx[:, None] <= idx[None, :], 0.0, NEGM).astype(np.float32)
    mprev = np.where(idx[:, None] > idx[None, :], 0.0, NEGM).astype(np.float32)
    return np.concatenate([ident, tri, ones, mcur, mprev], axis=1)


def rope_table(T):
    NT = T + 1
    L = N_META + T * 128
    pos = np.arange(L, dtype=np.float32)
    inv_freq = (np.float32(500000.0) ** (-np.arange(0, 16, 2, dtype=np.float32) / np.float32(16))).astype(np.float32)
    ang = (pos[:, None] * inv_freq[None, :]).astype(np.float32)
    c_, s_ = np.cos(ang), np.sin(ang)
    cs = np.concatenate([c_, c_, -s_, s_], axis=1).astype(np.float32)
    out = np.zeros((NT, 128, 32), np.float32)
    out[0, :16] = cs[:16]
    out[1:] = cs[16:].reshape(T, 128, 32)
    return out


def prep_inputs(inp, T, depth, batch_sel):
    f = lambda a: np.ascontiguousarray(np.asarray(a, dtype=np.float32))
    rep = lambda a: np.ascontiguousarray(np.broadcast_to(f(a).reshape(1, -1), (128, f(a).size)))
    fm = lambda a: np.ascontiguousarray(f(a).reshape(depth, 8, 128).transpose(2, 0, 1).reshape(128, depth * 8))
    common = {
        "meta": f(inp["meta"]),
        "w_in": f(inp["w_in"]),
        "w_out": f(inp["w_out"]),
        "ng_fm": fm(inp["norm_g"]),
        "og_fm": fm(inp["mlstm_out_norm_g"]),
        "gq_b": rep(inp["attn_q_norm_g"]),
        "gk_b": rep(inp["attn_k_norm_g"]),
        "sink_b": rep(inp["attn_sink"]),
        "convw_fm": np.ascontiguousarray(f(inp["mlstm_conv_w"]).reshape(depth, 4, 8, 128).transpose(3, 0, 1, 2).reshape(128, depth * 32)),
        "convb_row": f(inp["mlstm_conv_b"]).reshape(1, depth * 1024),
        "bif_b": rep(np.concatenate([f(inp["mlstm_b_i"]), f(inp["mlstm_b_f"])], axis=1)),
        "rope": rope_table(T),
        "cst": host_consts(),
    }
    x = f(inp["x"])
    maps = []
    for b in batch_sel:
        m = dict(common)
        m["x"] = np.ascontiguousarray(x[b])
        maps.append(m)
    return maps


_NC_CACHE = {}


def kernel(**inputs):
    x = np.asarray(inputs["x"])
    B, SEQ, _ = x.shape
    T = SEQ // 128
    depth = np.asarray(inputs["w_in"]).shape[0]
    key = (T, depth)
    if key not in _NC_CACHE:
        _NC_CACHE[key] = build(T, depth)
    nc = _NC_CACHE[key]
    n_cores = 8
    active = [0, 1, 4, 5]
    real = prep_inputs(inputs, T, depth, list(range(B)))
    zero = {k: np.zeros_like(v) for k, v in real[0].items()}
    maps = [zero] * n_cores
    maps = list(maps)
    for b, c in enumerate(active[:B]):
        maps[c] = real[b]
    res = run_bass_kernel_spmd(nc, maps, core_ids=list(range(n_cores)))
    out = np.stack([np.asarray(res.results[active[b]]["y"]) for b in range(B)], axis=0)
    return out.astype(np.float32)
```

```python
import math
import numpy as np
from contextlib import ExitStack
import concourse.bass as bass
import concourse.mybir as mybir
from concourse.bass_utils import run_bass_kernel_spmd

F32 = mybir.dt.float32
BF16 = mybir.dt.bfloat16
AF = mybir.ActivationFunctionType
ALU = mybir.AluOpType
AX = mybir.AxisListType

D_MODEL = 1024
N_META = 16
IN_W = 6408
EPS = 1e-6
NEGM = -30000.0
LN_HALF = math.log(0.5)
LN_CK = math.log(0.5 / math.sqrt(128.0))

SEGS = [(0, 512, 0), (512, 512, 512), (1024, 256, 1024), (4352, 8, 1280), (1280, 512, 1288), (1792, 512, 1800),
        (2304, 512, 2312), (2816, 512, 2824), (3328, 512, 3336), (3840, 512, 3848),
        (4360, 512, 4360), (4872, 512, 4872), (5384, 512, 5384), (5896, 512, 5896)]
SEG_OF = {"q0": [0], "q1": [1], "kv": [2, 3], "ag0": [4], "ag1": [5], "mq": [6], "mk": [7], "mv0": [8], "mv1": [9],
          "mo0": [10], "mo1": [11], "mg0": [12], "mg1": [13]}


class Buf:
    __slots__ = ("name", "w", "r")

    def __init__(self, name):
        self.name = name
        self.w = None
        self.r = {}


class Sched:
    NDMA = 24

    def __init__(self, nc, es):
        self.nc = nc
        self.engs = {"pe": nc.tensor, "act": nc.scalar, "dve": nc.vector, "pool": nc.gpsimd, "sp": nc.sync}
        self.sems = {}
        self.cnt = {}
        for k in self.engs:
            self.sems[k] = es.enter_context(nc.semaphore("s_" + k))
            self.cnt[k] = 0
        self.dsems = [es.enter_context(nc.semaphore(f"s_dma{i}")) for i in range(self.NDMA)]
        self.dval = [0] * self.NDMA
        self.dnext = 0
        self.waited = {k: {} for k in self.engs}
        self.ninst = 0
        self.nwaits = 0
        self.tag = ''
        self.log = {}

    def _sem(self, key):
        return self.sems[key] if isinstance(key, str) else self.dsems[key]

    def _wait(self, e, key, val):
        if key == e and e == "pe":
            return
        w = self.waited[e]
        if w.get(key, 0) >= val:
            return
        w[key] = val
        self.engs[e].wait_ge(self._sem(key), val)
        self.nwaits += 1

    def _deps(self, e, reads, writes):
        for b in reads:
            if b.w is not None:
                self._wait(e, *b.w)
        for b in writes:
            if b.w is not None:
                self._wait(e, *b.w)
            for k, v in b.r.items():
                self._wait(e, k, v)

    def _mark(self, tok, reads, writes):
        for b in reads:
            if b.r.get(tok[0], 0) < tok[1]:
                b.r[tok[0]] = tok[1]
        for b in writes:
            b.w = tok
            b.r = {}

    def op(self, e, fn, reads=(), writes=(), banks=()):
        writes = list(writes) + list(banks)
        self._deps(e, reads, writes)
        ins = fn(self.engs[e])
        self.cnt[e] += 1
        ins.then_inc(self.sems[e], 1)
        self._mark((e, self.cnt[e]), reads, writes)
        if self.tag is not None:
            self.log[(e, self.cnt[e])] = self.tag
        self.ninst += 1
        return ins

    def group(self, e, fns, reads=(), writes=(), banks=()):
        writes = list(writes) + list(banks)
        self._deps(e, reads, writes)
        ins = None
        for fn in fns:
            ins = fn(self.engs[e])
            self.ninst += 1
        self.cnt[e] += 1
        ins.then_inc(self.sems[e], 1)
        self._mark((e, self.cnt[e]), reads, writes)
        if self.tag is not None:
            self.log[(e, self.cnt[e])] = self.tag
        return ins

    def dma(self, out, in_, reads=(), writes=(), q="sp"):
        k = self.dnext
        self.dnext = (self.dnext + 1) % self.NDMA
        if self.dval[k] > 0:
            self._wait(q, k, self.dval[k])
        self._deps(q, reads, writes)
        ins = self.engs[q].dma_start(out=out, in_=in_)
        self.dval[k] += 16
        ins.then_inc(self.dsems[k], 16)
        self._mark((k, self.dval[k]), reads, writes)
        self.ninst += 1
        return ins

    def drain_dmas(self, q="sp"):
        for k in range(self.NDMA):
            if self.dval[k]:
                self._wait(q, k, self.dval[k])


def build(T, depth):
    NT = T + 1
    nc = bass.Bass("TRN2", target_bir_lowering=False)
    dt = lambda name, shape, kind="ExternalInput", dtype=F32: nc.dram_tensor(name, shape, dtype, kind=kind).ap()
    x_d = dt("x", [T * 128, D_MODEL])
    meta_d = dt("meta", [N_META, D_MODEL])
    win_d = dt("w_in", [depth, D_MODEL, IN_W])
    wout_d = dt("w_out", [depth, 2048, D_MODEL])
    ng_d = dt("ng_fm", [128, depth * 8])
    og_d = dt("og_fm", [128, depth * 8])
    gq_d = dt("gq_b", [128, depth * 64])
    gk_d = dt("gk_b", [128, depth * 64])
    sink_d = dt("sink_b", [128, depth * 16])
    convw_d = dt("convw_fm", [128, depth * 32])
    convb_d = dt("convb_row", [1, depth * 1024])
    bif_d = dt("bif_b", [128, depth * 8])
    rope_d = dt("rope", [NT, 128, 32])
    cst_d = dt("cst", [128, 640])
    y_d = dt("y", [T * 128, D_MODEL], kind="ExternalOutput")
    xs_d = dt("xs", [NT, 128, D_MODEL], kind="Internal")

    with ExitStack() as es:
        S = Sched(nc, es)
        sb = lambda name, shape, dtype=F32: es.enter_context(nc.sbuf_tensor(name, shape, dtype))
        ps = lambda name, shape, dtype=F32: es.enter_context(nc.psum_tensor(name, shape, dtype))

        win = sb("win", [128, 8, IN_W], BF16)
        wout = sb("wout", [128, 16, 1024], BF16)
        WINB = {(k, s): Buf(f"win{k}_{s}") for k in range(8) for s in range(len(SEGS))}
        WOUTB = {(k, g): Buf(f"wout{k}_{g}") for k in range(16) for g in range(2)}
        stage = [sb(f"stage{i}", [128, 512]) for i in range(2)]
        STAGE = [Buf(f"stage{i}") for i in range(2)]
        cst = sb("cst_sb", [128, 384]); CST = Buf("cst")
        ident_f, tri_f, ones_f = cst[:, 0:128], cst[:, 128:256], cst[:, 256:384]
        ident_bf = sb("ident_bf", [128, 128], BF16)
        mask01 = sb("mask01", [128, 128], BF16)
        mcur4 = sb("mcur4", [128, 4, 128], BF16)
        mprev4 = sb("mprev4", [128, 4, 128], BF16)
        onesrow = sb("onesrow", [1, 128], BF16)
        CONSTB = Buf("constb")
        ng = sb("ng", [128, depth * 8]); og = sb("og", [128, depth * 8])
        gq = sb("gq", [128, 64]); gk = sb("gk", [128, 64]); PARL = Buf("parl")
        sink = sb("sink", [128, 16]); esink = sb("esink", [128, 16]); ESINK = Buf("esink")
        convw = sb("convw", [128, depth * 32])
        convb_bf = sb("convb_bf", [1, 1024], BF16); CONVB = Buf("convb")
        bif = sb("bif", [128, depth * 8])
        PAR = Buf("params")
        diag = sb("diag", [128, 32, 128], BF16); DIAG = Buf("diag")
        neghalf = sb("neghalf", [128, 18]);
        xt = [sb(f"xt{i}", [128, 1024]) for i in range(2)]; XT = [Buf(f"xt{i}") for i in range(2)]
        ropet = [sb(f"ropet{i}", [128, 32]) for i in range(2)]; ROPE = [Buf(f"rope{i}") for i in range(2)]
        ss = sb("ss", [128, 1]); SS = Buf("ss")
        rstd = sb("rstd", [128, 1]); RSTD = Buf("rstd")

        hT = sb("hT", [128, 8, 128], BF16); HT = [Buf("hT0"), Buf("hT1")]
        junk = sb("junk", [128, 1024], BF16); JUNK = [Buf("junk0"), Buf("junk1")]
        ssq = sb("ssq", [128, 18]); SSQ = [Buf("ssq0"), Buf("ssq1")]
        rq = sb("rq", [128, 18]); RQ = [Buf("rq0"), Buf("rq1")]
        qb = sb("qb", [128, 18, 64], BF16); QB = [Buf("qb0"), Buf("qb1")]
        rtmp = stage[0][:, :].rearrange("p (a b) -> p a b", a=4); RTMP = STAGE[0]
        qT = sb("qT", [128, 16, 128], BF16); QT = [Buf(f"qT{i}") for i in range(4)]
        kvg = stage[1]; KVG = STAGE[1]
        SSK = Buf("ssk")
        RK = Buf("rk")
        kb = qb[:, 16:18, :]; KB = Buf("kb")
        kT = [sb(f"kT{i}", [128, 2, 128], BF16) for i in range(2)]; KT = [Buf(f"kT{i}") for i in range(2)]
        kTm = sb("kTm", [128, 2, 16], BF16); KTM = Buf("kTm")
        vaug = [sb(f"vaug{i}", [128, 2, 65], BF16) for i in range(2)]; VAUG = [Buf(f"vaug{i}") for i in range(2)]
        vm = sb("vm", [128, 2, 65], BF16); VM = Buf("vm")
        G = sb("G", [128, 1024], BF16); GB = [Buf("G0"), Buf("G1")]
        xn = G
        probs = sb("probs", [128, 3, 512], BF16); PROBS = [Buf(f"probs{i}") for i in range(3)]
        den = sb("den", [128, 4]); DEN = Buf("den")
        y = sb("y_sb", [128, 2048], BF16); YA = [Buf(f"ya{i}") for i in range(4)]; YM = [Buf(f"ym{i}") for i in range(4)]
        yT = sb("yT", [128, 16, 128], BF16); YT = [Buf(f"yT{i}") for i in range(4)]
        xqk = sb("xqk", [128, 8, 132], BF16); XQK = [Buf("xqk0"), Buf("xqk1")]

        qkm = sb("qkm", [128, 8, 128], BF16); QKM = [Buf("qkm0"), Buf("qkm1")]
        ktok = sb("ktok", [128, 4, 128], BF16); KTOK = Buf("ktok")
        gl = sb("gl", [128, 8]); GL = Buf("gl")
        gt = sb("gt", [128, 4]); GT = Buf("gt")
        dv = sb("dv", [128, 4]); DV = Buf("dv")
        wv = sb("wv", [128, 4]); WV = Buf("wv")
        uv = sb("uv", [128, 4]); UV = Buf("uv")
        ev = [sb(f"ev{i}", [128, 4]) for i in range(2)]; EV = [Buf(f"ev{i}") for i in range(2)]
        vp = sb("vp", [128, 4, 258], BF16); VP = [Buf(f"vp{i}") for i in range(4)]
        Sm = sb("Sm", [128, 4, 128], BF16); SM = [Buf(f"Sm{i}") for i in range(4)]
        thc = Sm
        Dst = sb("Dst", [128, 4, 257]); DST = [Buf(f"D{i}") for i in range(4)]
        Cbf = sb("Cbf", [128, 4, 257], BF16); CBF = [Buf(f"Cbf{i}") for i in range(4)]
        to = sb("to", [128, 1024], BF16); TO = [Buf("to0"), Buf("to1")]
        tgm = junk; TGM = JUNK
        hs = sb("hs", [128, 8, 4]); HS = [Buf(f"hs{i}") for i in range(4)]

        pz = [ps(f"pz{i}", [128, 512]) for i in range(3)]; PZ = [Buf(f"pz{i}") for i in range(3)]
        pt = [ps(f"pt{i}", [128, 1024], BF16) for i in range(2)]; PT = [Buf(f"pt{i}") for i in range(2)]
        pa = ps("pa", [128, 512]); PA = Buf("pa")
        pb = ps("pb", [128, 512]); PB = Buf("pb")
        pc = ps("pc", [128, 512]); PC = Buf("pc")
        zctr = [0]
        tctr = [0]

        def zbank():
            i = zctr[0] % 3
            zctr[0] += 1
            return pz[i], PZ[i]

        def tbank():
            i = tctr[0] % 2
            tctr[0] += 1
            return pt[i], PT[i]

        S.dma(cst[:], cst_d[:, 0:384], writes=[CST])
        S.dma(stage[0][:, 0:128], cst_d[:, 384:512], writes=[STAGE[0]])
        S.dma(stage[1][:, 0:128], cst_d[:, 512:640], writes=[STAGE[1]])
        for (t_sb, t_d) in [(ng, ng_d), (og, og_d), (convw, convw_d), (bif, bif_d)]:
            S.dma(t_sb[:], t_d[:, :], writes=[PAR])
        S.op("dve", lambda e: e.tensor_copy(out=ident_bf[:], in_=ident_f), reads=[CST], writes=[CONSTB])
        S.op("dve", lambda e: e.tensor_copy(out=mask01[:], in_=tri_f), reads=[CST], writes=[CONSTB])
        for i in range(4):
            S.op("dve", lambda e, i=i: e.tensor_copy(out=mcur4[:, i, :], in_=stage[0][:, 0:128]), reads=[STAGE[0]], writes=[CONSTB])
            S.op("dve", lambda e, i=i: e.tensor_copy(out=mprev4[:, i, :], in_=stage[1][:, 0:128]), reads=[STAGE[1]], writes=[CONSTB])
        S.op("dve", lambda e: e.tensor_copy(out=onesrow[:], in_=cst[0:1, 256:384]), reads=[CST], writes=[CONSTB])
        S.op("pool", lambda e: e.memset(neghalf[:], -0.5), writes=[CONSTB])
        for i in range(2):
            S.op("pool", lambda e, i=i: e.memset(vaug[i][:, :, 64:65], 1.0), writes=[VAUG[i]])
        S.op("pool", lambda e: e.memset(vm[:, :, 64:65], 1.0), writes=[VM])

        XT1H = [Buf('xt1a'), Buf('xt1b')]
        wst = [(stage[0], STAGE[0]), (stage[1], STAGE[1]), (xt[1][:, 0:512], XT1H[0]), (xt[1][:, 512:1024], XT1H[1])]

        def load_layer_weights(l):
            i = 0
            engs = ["dve", "act", "pool"]
            for k in range(8):
                for si, (d0, w, s0) in enumerate(SEGS):
                    st, STB = wst[i % 4]
                    S.dma(st[:, :w], win_d[l, k * 128:(k + 1) * 128, d0:d0 + w], writes=[STB] + ([XT[1]] if i in (2, 3) else []))
                    eng = engs[i % 3]
                    sc = ng[:, l * 8 + k:l * 8 + k + 1]
                    o_ap = win[:, k, s0:s0 + w]
                    if eng == "dve":
                        S.op("dve", lambda e, o_ap=o_ap, st=st, w=w, sc=sc: e.tensor_scalar(out=o_ap, in0=st[:, :w], scalar1=sc, scalar2=None, op0=ALU.mult),
                             reads=[STB, PAR], writes=[WINB[(k, si)]])
                    elif eng == "act":
                        S.op("act", lambda e, o_ap=o_ap, st=st, w=w, sc=sc: e.activation(out=o_ap, in_=st[:, :w], func=AF.Copy, scale=sc),
                             reads=[STB, PAR], writes=[WINB[(k, si)]])
                    else:
                        S.op("pool", lambda e, o_ap=o_ap, st=st, w=w, sc=sc: e.tensor_scalar(out=o_ap, in0=st[:, :w], scalar1=sc, scalar2=0.0, op0=ALU.mult, op1=ALU.add),
                             reads=[STB, PAR], writes=[WINB[(k, si)]])
                    i += 1
            for k in range(16):
                for g in range(2):
                    st, STB = wst[i % 4]
                    S.dma(st[:, :512], wout_d[l, k * 128:(k + 1) * 128, g * 512:(g + 1) * 512], writes=[STB])
                    eng = engs[i % 3]
                    o_ap = wout[:, k, g * 512:(g + 1) * 512]
                    if k < 8:
                        if eng == "act":
                            S.op("act", lambda e, o_ap=o_ap, st=st: e.copy(out=o_ap, in_=st[:, :512]), reads=[STB], writes=[WOUTB[(k, g)]])
                        else:
                            S.op(eng, lambda e, o_ap=o_ap, st=st: e.tensor_copy(out=o_ap, in_=st[:, :512]), reads=[STB], writes=[WOUTB[(k, g)]])
                    else:
                        sc = og[:, l * 8 + k - 8:l * 8 + k - 7]
                        if eng == "dve":
                            S.op("dve", lambda e, o_ap=o_ap, st=st, sc=sc: e.tensor_scalar(out=o_ap, in0=st[:, :512], scalar1=sc, scalar2=None, op0=ALU.mult),
                                 reads=[STB, PAR], writes=[WOUTB[(k, g)]])
                        elif eng == "act":
                            S.op("act", lambda e, o_ap=o_ap, st=st, sc=sc: e.activation(out=o_ap, in_=st[:, :512], func=AF.Copy, scale=sc),
                                 reads=[STB, PAR], writes=[WOUTB[(k, g)]])
                        else:
                            S.op("pool", lambda e, o_ap=o_ap, st=st, sc=sc: e.tensor_scalar(out=o_ap, in0=st[:, :512], scalar1=sc, scalar2=0.0, op0=ALU.mult, op1=ALU.add),
                                 reads=[STB, PAR], writes=[WOUTB[(k, g)]])
                    i += 1
            for c in range(8):
                for tap in range(4):
                    sc = convw[:, l * 32 + tap * 8 + c:l * 32 + tap * 8 + c + 1]
                    S.op("pool", lambda e, c=c, tap=tap, sc=sc: e.tensor_scalar(out=diag[:, c * 4 + tap, :], in0=ident_f, scalar1=sc, scalar2=0.0, op0=ALU.mult, op1=ALU.add),
                         reads=[CST, PAR], writes=[DIAG])
            S.dma(gq[:], gq_d[:, l * 64:(l + 1) * 64], writes=[PARL])
            S.dma(gk[:], gk_d[:, l * 64:(l + 1) * 64], writes=[PARL])
            S.dma(sink[:], sink_d[:, l * 16:(l + 1) * 16], writes=[PARL])
            S.dma(xt[1][0:1, :], convb_d[0:1, l * 1024:(l + 1) * 1024], writes=[XT[1], XT1H[0], XT1H[1]])
            S.op("dve", lambda e: e.tensor_copy(out=convb_bf[:], in_=xt[1][0:1, :]), reads=[XT[1]], writes=[CONVB])
            S.op("act", lambda e: e.activation(out=esink[:], in_=sink[:, :], func=AF.Exp, bias=math.log(2.0)), reads=[PARL], writes=[ESINK])
            S.op("pool", lambda e: e.memset(xqk[:, :, 0:3], 0.0), writes=XQK)

        def load_x(l, j):
            par = j % 2
            n = 16 if j == 0 else 128
            if l == 0:
                if j == 0:
                    S.dma(xt[par][:16, :], meta_d[:, :], writes=[XT[par]])
                else:
                    S.dma(xt[par][:, :], x_d[(j - 1) * 128:j * 128, :], writes=[XT[par]])
            else:
                S.dma(xt[par][:n, :], xs_d[j, :n, :], reads=[XS[j]], writes=[XT[par]])
            S.dma(ropet[par][:n, :], rope_d[j, :n, :], writes=[ROPE[par]])

        XS = [Buf(f"xs{j}") for j in range(NT)]

        def do_rope_all(n, par):
            nh = 18
            x16 = qb[:n, :, 0:16]
            x1 = qb[:n, :, 0:8]
            x2 = qb[:n, :, 8:16]
            ta = stage[0][:n, 0:288].rearrange("p (h d) -> p h d", d=16)
            tb_ = stage[0][:n, 288:432].rearrange("p (h d) -> p h d", d=8)
            tc_ = stage[1][:n, 264:408].rearrange("p (h d) -> p h d", d=8)
            cc = ropet[par][:n, 0:16].unsqueeze(1).broadcast_to([n, nh, 16])
            ms = ropet[par][:n, 16:24].unsqueeze(1).broadcast_to([n, nh, 8])
            ps_ = ropet[par][:n, 24:32].unsqueeze(1).broadcast_to([n, nh, 8])
            R = [ROPE[par], QB[0], QB[1], KB]
            S.op("pool", lambda e: e.tensor_tensor(out=ta, in0=x16, in1=cc, op=ALU.mult), reads=R, writes=[RTMP])
            S.op("pool", lambda e: e.tensor_tensor(out=tb_, in0=x2, in1=ms, op=ALU.mult), reads=R, writes=[RTMP])
            S.op("pool", lambda e: e.tensor_tensor(out=tc_, in0=x1, in1=ps_, op=ALU.mult), reads=R, writes=[KVG])
            S.op("pool", lambda e: e.tensor_tensor(out=x1, in0=ta[:, :, 0:8], in1=tb_, op=ALU.add), reads=[RTMP], writes=[QB[0], QB[1], KB])
            S.op("pool", lambda e: e.tensor_tensor(out=x2, in0=ta[:, :, 8:16], in1=tc_, op=ALU.add), reads=[RTMP, KVG], writes=[QB[0], QB[1], KB])

        def zgroup(n, name, c0, w, evac, bank=None):
            S.tag = S.tag.split('/')[0] + '/' + name
            zb, ZB = zbank() if bank is None else bank
            rd = [HT[0], HT[1]] + [WINB[(k, s)] for k in range(8) for s in SEG_OF[name]]
            S.group("pe", [lambda e, k=k: e.matmul(out=zb[:n, :w], lhsT=hT[:, k, :n], rhs=win[:, k, c0:c0 + w], start=(k == 0), stop=(k == 7))
                           for k in range(8)], reads=rd, banks=[ZB])
            evac(zb, ZB)

        def pre_fn(l, j):
            n = 16 if j == 0 else 128
            par = j % 2
            xt_, X = xt[par], XT[par]
            S.op("act", lambda e: e.activation(out=junk[:n, :], in_=xt_[:n, :], func=AF.Square, accum_out=ss[:n, :]), reads=[X], writes=[JUNK[0], JUNK[1], SS])
            S.op("pool", lambda e: e.tensor_scalar(out=rstd[:n, :], in0=ss[:n, :], scalar1=1.0 / D_MODEL, scalar2=EPS, op0=ALU.mult, op1=ALU.add), reads=[SS], writes=[RSTD])
            S.op("pool", lambda e: e.tensor_tensor(out=rstd[:n, :], in0=rstd[:n, :], in1=neghalf[:n, 0:1], op=ALU.pow), reads=[RSTD, CONSTB], writes=[RSTD])
            S.op("dve", lambda e: e.tensor_scalar(out=xn[:n, :], in0=xt_[:n, :], scalar1=rstd[:n, 0:1], scalar2=None, op0=ALU.mult), reads=[X, RSTD], writes=GB)

        def ht_fn(l, j):
            n = 16 if j == 0 else 128
            for hb in range(2):
                tb, TB = tbank()
                S.group("pe", [lambda e, k=k, tb=tb: e.transpose(out=tb[:, (k % 4) * 128:(k % 4) * 128 + n], in_=xn[:n, k * 128:(k + 1) * 128], identity=ident_bf[:n, :n])
                               for k in range(4 * hb, 4 * hb + 4)], reads=GB + [CONSTB], banks=[TB])
                src = tb[:, 0:512].rearrange("p (c t) -> p c t", c=4)[:, :, :n]
                if hb == 0:
                    S.op("act", lambda e, src=src: e.copy(out=hT[:, 0:4, :n], in_=src), banks=[TB], writes=[HT[0]])
                else:
                    S.op("dve", lambda e, src=src: e.tensor_copy(out=hT[:, 4:8, :n], in_=src), banks=[TB], writes=[HT[1]])

        def tile_fn(l, j, last_layer, nxt):
            n = 16 if j == 0 else 128
            par = j % 2
            xt_, X = xt[par], XT[par]
            if j + 1 < NT:
                load_x(l, j + 1)
            vdst, VD = (vm, VM) if j == 0 else (vaug[par], VAUG[par])
            ktd, KTD = (kTm, KTM) if j == 0 else (kT[par], KT[par])

            S.tag = f'L{l}T{j}:z'
            def mk_ev_ag(gi):
                def ev(zb, ZB):
                    S.op("act", lambda e: e.activation(out=G[:n, gi * 512:(gi + 1) * 512], in_=zb[:n, :], func=AF.Tanh, scale=0.5), banks=[ZB], writes=[GB[gi]])
                    S.op("dve", lambda e: e.scalar_tensor_tensor(out=G[:n, gi * 512:(gi + 1) * 512], in0=G[:n, gi * 512:(gi + 1) * 512], scalar=1.0, in1=zb[:n, :], op0=ALU.add, op1=ALU.mult), banks=[ZB], writes=[GB[gi]])
                return ev

            def after_kv2b():
                S.op("act", lambda e: e.activation(out=uv[:n, :], in_=dv[:n, :], func=AF.Exp, bias=LN_CK), reads=[DV], writes=[UV])

            def after_kv2():
                S.group("pe", [lambda e: e.matmul(out=pa[:n, 0:4], lhsT=tri_f[:n, :n], rhs=gt[:n, :], start=True, stop=True),
                               lambda e: e.matmul(out=pa[:, 8:12], lhsT=ones_f[:n, :], rhs=gt[:n, :], start=True, stop=True)],
                        reads=[CST, GT], banks=[PA])
                S.op("act", lambda e: e.activation(out=wv[:n, :], in_=pa[:n, 0:4], func=AF.Exp, scale=-1.0, bias=LN_HALF), banks=[PA], writes=[WV])
                S.op("act", lambda e: e.activation(out=ev[par][:, :], in_=pa[:, 8:12], func=AF.Exp, scale=-1.0), banks=[PA], writes=[EV[par]])
                S.op("dve", lambda e: e.tensor_tensor(out=dv[:n, :], in0=pa[:n, 0:4], in1=gl[:n, 0:4], op=ALU.add), banks=[PA], reads=[GL], writes=[DV])

            def mk_ev_mv(gi):
                def ev(zb, ZB):
                    for hh in range(2):
                        h = 2 * gi + hh
                        S.op("act", lambda e, h=h, hh=hh: e.activation(out=vp[:n, h, 0:256], in_=zb[:n, hh * 256:(hh + 1) * 256], func=AF.Copy, scale=uv[:n, h:h + 1]), banks=[ZB], reads=[UV], writes=[VP[h]])
                        S.op("dve", lambda e, h=h: e.tensor_copy(out=vp[:n, h, 256:257], in_=uv[:n, h:h + 1]), reads=[UV], writes=[VP[h]])
                return ev

            def mk_ev_mo(gi):
                def ev(zb, ZB):
                    S.op("act", lambda e: e.activation(out=to[:n, gi * 512:(gi + 1) * 512], in_=zb[:n, :], func=AF.Tanh, scale=0.5), banks=[ZB], writes=[TO[gi]])
                return ev

            def mk_ev_mg(gi):
                def ev(zb, ZB):
                    sl = slice(gi * 512, (gi + 1) * 512)
                    S.op("act", lambda e: e.activation(out=tgm[:n, sl], in_=zb[:n, :], func=AF.Tanh, scale=0.5), banks=[ZB], writes=[TGM[gi]])
                    S.op("dve", lambda e: e.scalar_tensor_tensor(out=tgm[:n, sl], in0=tgm[:n, sl], scalar=1.0, in1=zb[:n, :], op0=ALU.add, op1=ALU.mult), banks=[ZB], writes=[TGM[gi]])
                    S.op("dve", lambda e: e.scalar_tensor_tensor(out=to[:n, sl], in0=to[:n, sl], scalar=1.0, in1=tgm[:n, sl], op0=ALU.add, op1=ALU.mult), reads=[TGM[gi]], writes=[TO[gi]])
                return ev

            def fm_z(half):
                seg = "mq" if half == 0 else "mk"
                c0 = 2312 + half * 512

                def ev(zb, ZB):
                    if half == 0:
                        S.op("act", lambda e: e.copy(out=thc[:n, :, :], in_=zb[:n, :].rearrange("p (c d) -> p c d", c=4)), banks=[ZB], writes=SM)
                    else:
                        S.op("act", lambda e: e.copy(out=qkm[:n, 4:8, :], in_=zb[:n, :].rearrange("p (c d) -> p c d", c=4)), banks=[ZB], writes=[QKM[1]])
                zgroup(n, seg, c0, 512, ev)

            def fm_T(half):
                tb, TB = tbank()
                srcs = thc if half == 0 else qkm
                off = 0 if half == 0 else 4
                S.group("pe", [lambda e, cc=cc, tb=tb: e.transpose(out=tb[:, cc * 128:cc * 128 + n], in_=srcs[:n, off + cc, :], identity=ident_bf[:n, :n]) for cc in range(4)],
                        reads=(SM if half == 0 else [QKM[1]]) + [CONSTB], banks=[TB])
                src = tb[:, 0:512].rearrange("p (c t) -> p c t", c=4)[:, :, :n]
                if half == 0:
                    S.op("act", lambda e, src=src: e.copy(out=xqk[:, 0:4, 3:3 + n], in_=src), banks=[TB], writes=[XQK[0]])
                else:
                    S.op("act", lambda e, src=src: e.copy(out=xqk[:, 4:8, 3:3 + n], in_=src), banks=[TB], writes=[XQK[1]])

            def conv_half(half):
                pcv, PCV = zbank()
                fns = []
                for cc in range(4):
                    c = half * 4 + cc
                    for tap in range(4):
                        fns.append(lambda e, cc=cc, c=c, tap=tap: e.matmul(out=pcv[:, cc * 128:cc * 128 + n], lhsT=diag[:, c * 4 + tap, :], rhs=xqk[:, c, tap:tap + n], start=(tap == 0), stop=False))
                    fns.append(lambda e, cc=cc, c=c: e.matmul(out=pcv[:, cc * 128:cc * 128 + n], lhsT=convb_bf[0:1, c * 128:(c + 1) * 128], rhs=onesrow[0:1, :n], start=False, stop=True))
                S.group("pe", fns, reads=[DIAG, XQK[half], CONVB, CONSTB], banks=[PCV])
                src = pcv[:, :].rearrange("p (c t) -> p c t", c=4)[:, :, :n]
                S.op("act", lambda e, src=src: e.activation(out=thc[:, :, :n], in_=src, func=AF.Tanh, scale=0.5), banks=[PCV], writes=SM)
                S.op("dve", lambda e, src=src, half=half: e.scalar_tensor_tensor(out=qkm[:, 4 * half:4 * half + 4, :n], in0=thc[:, :, :n], scalar=1.0, in1=src, op0=ALU.add, op1=ALU.mult), banks=[PCV], reads=SM, writes=[QKM[half]])
                S.op("pool", lambda e, half=half: e.tensor_copy(out=xqk[:, 4 * half:4 * half + 4, 0:3], in_=xqk[:, 4 * half:4 * half + 4, n:n + 3]), writes=[XQK[half]])

            def kT_fn():
                tb, TB = tbank()
                S.group("pe", [lambda e, g=g, tb=tb: e.transpose(out=tb[:64, g * 128:g * 128 + n], in_=kb[:n, g, :], identity=ident_bf[:n, :n]) for g in range(2)],
                        reads=[KB, CONSTB], banks=[TB])
                src = tb[:64, 0:256].rearrange("p (g t) -> p g t", g=2)[:, :, :n]
                S.op("act", lambda e, src=src: e.copy(out=ktd[:64, :, :n], in_=src), banks=[TB], writes=[KTD])

            def qT_fn(qq):
                tb, TB = tbank()
                S.group("pe", [lambda e, i=i, tb=tb: e.transpose(out=tb[:64, i * 128:i * 128 + n], in_=qb[:n, qq * 4 + i, :], identity=ident_bf[:n, :n]) for i in range(4)],
                        reads=[QB[0], QB[1], CONSTB], banks=[TB])
                src = tb[:64, 0:512].rearrange("p (g t) -> p g t", g=4)[:, :, :n]
                if qq % 2 == 0:
                    S.op("act", lambda e, src=src, qq=qq: e.copy(out=qT[:64, qq * 4:qq * 4 + 4, :n], in_=src), banks=[TB], writes=[QT[qq]])
                else:
                    S.op("dve", lambda e, src=src, qq=qq: e.tensor_copy(out=qT[:64, qq * 4:qq * 4 + 4, :n], in_=src), banks=[TB], writes=[QT[qq]])

            def attn_groups(u):
                g = u // 2
                groups = []
                groups.append((pa, PA, n, ktd[:64, g, :n], mcur4, vdst, VD, KTD))
                if j >= 2:
                    groups.append((pb, PB, 128, kT[1 - par][:64, g, :], mprev4, vaug[1 - par], VAUG[1 - par], KT[1 - par]))
                if j >= 1:
                    groups.append((pc, PC, 16, kTm[:64, g, :], None, vm, VM, KTM))
                return groups

            def attn_s(u):
                rq_ap = qT[:64, u * 4:u * 4 + 4, :n]
                for gi_, (pp, PP, nk, kl, msk, vt, VB, KBF) in enumerate(attn_groups(u)):
                    out_ap = pp[:nk, 0:4 * n].rearrange("p (h t) -> p h t", h=4)
                    fns = [lambda e, out_ap=out_ap, kl=kl, msk=msk: e.matmul(out=out_ap, lhsT=kl, rhs=rq_ap, start=True, stop=(msk is None))]
                    if msk is not None:
                        fns.append(lambda e, out_ap=out_ap, msk=msk, nk=nk: e.matmul(out=out_ap, lhsT=ident_bf[:nk, :nk], rhs=msk[:nk, :, :n], start=False, stop=True))
                    S.group("pe", fns, reads=[QT[u], KBF, CONSTB], banks=[PP])
                    S.op("act", lambda e, pp=pp, nk=nk, gi_=gi_: e.activation(out=probs[:nk, gi_, 0:4 * n], in_=pp[:nk, 0:4 * n], func=AF.Exp, scale=0.125), banks=[PP], writes=[PROBS[gi_]])

            def attn_pv(u):
                g = u // 2
                groups = attn_groups(u)
                zb, ZB = zbank()
                fns = []
                for i in range(4):
                    for gi_, (pp, PP, nk, kl, msk, vt, VB, KBF) in enumerate(groups):
                        fns.append(lambda e, i=i, gi_=gi_, nk=nk, vt=vt: e.matmul(out=zb[:n, i * 65:(i + 1) * 65], lhsT=probs[:nk, gi_, i * n:(i + 1) * n], rhs=vt[:nk, g, :],
                                                                              start=(gi_ == 0), stop=(gi_ == len(groups) - 1)))
                S.group("pe", fns, reads=[PROBS[i] for i in range(len(groups))] + [gr[6] for gr in groups], banks=[ZB])
                o3 = zb[:n, 0:260].rearrange("p (h d) -> p h d", h=4)
                S.op("dve", lambda e, o3=o3: e.scalar_tensor_tensor(out=den[:n, :], in0=o3[:, :, 64], scalar=2.0, in1=esink[:n, u * 4:u * 4 + 4], op0=ALU.mult, op1=ALU.add), banks=[ZB], reads=[ESINK], writes=[DEN])
                S.op("dve", lambda e: e.reciprocal(out=den[:n, :], in_=den[:n, :]), reads=[DEN], writes=[DEN])
                for i in range(4):
                    hd = u * 4 + i
                    S.op("dve", lambda e, i=i, hd=hd, o3=o3: e.scalar_tensor_tensor(out=y[:n, hd * 64:(hd + 1) * 64], in0=o3[:, i, 0:64], scalar=den[:n, i:i + 1], in1=G[:n, hd * 64:(hd + 1) * 64], op0=ALU.mult, op1=ALU.mult),
                         banks=[ZB], reads=[DEN, GB[hd // 8]], writes=[YA[u]])

            pbanks = [(pb, PB), (pc, PC), None, None]
            cbanks = [None, (pa, PA)]

            def ml_a():
                tb, TB = tbank()
                S.group("pe", [lambda e, h=h, tb=tb: e.transpose(out=tb[:n, h * 128:(h + 1) * 128], in_=qkm[:, 4 + h, :n], identity=ident_bf[:, :]) for h in range(4)],
                        reads=[QKM[1], CONSTB], banks=[TB])
                S.op("act", lambda e, tb=tb: e.copy(out=ktok[:n, :, :], in_=tb[:n, 0:512].rearrange("p (h d) -> p h d", h=4)), banks=[TB], writes=[KTOK])
                S.group("pe", [lambda e, h=h: e.matmul(out=pa[:n, h * 128:h * 128 + n], lhsT=qkm[:, 4 + h, :n], rhs=qkm[:, h, :n], start=True, stop=True) for h in range(4)],
                        reads=[QKM[0], QKM[1]], banks=[PA])
                S.op("dve", lambda e: e.tensor_tensor(out=Sm[:n, :, :n], in0=pa[:n, :].rearrange("p (h t) -> p h t", h=4)[:, :, :n], in1=mask01[:n, :n].unsqueeze(1).broadcast_to([n, 4, n]), op=ALU.mult),
                     banks=[PA], reads=[CONSTB], writes=SM)

            def ml_P():
                pbanks[2] = zbank()
                pbanks[3] = zbank()
                for h in range(4):
                    PP_, PPB = pbanks[h]
                    fns = [lambda e, h=h, PP_=PP_: e.matmul(out=PP_[:n, 0:257], lhsT=Sm[:n, h, :n], rhs=vp[:n, h, 0:257], start=True, stop=(j == 0))]
                    if j > 0:
                        fns.append(lambda e, h=h, PP_=PP_: e.matmul(out=PP_[:n, 0:257], lhsT=qkm[:, h, :n], rhs=Cbf[:, h, :], start=False, stop=True))
                    S.group("pe", fns, reads=[SM[h], VP[h], QKM[0], CBF[h]], banks=[PPB])

            def ml_C():
                cb = zbank()
                while cb[1] in (pbanks[2][1], pbanks[3][1]):
                    cb = zbank()
                cbanks[0] = cb
                for h in range(4):
                    zb, ZB = cbanks[h % 2]
                    S.group("pe", [lambda e, h=h, zb=zb: e.matmul(out=zb[:, 0:257], lhsT=ktok[:n, h, :], rhs=vp[:n, h, 0:257], start=True, stop=True)], reads=[KTOK, VP[h]], banks=[ZB])
                    if j == 0:
                        S.op("dve", lambda e, h=h, zb=zb: e.tensor_copy(out=Dst[:, h, :], in_=zb[:, 0:257]), banks=[ZB], writes=[DST[h]])
                    else:
                        S.op("dve", lambda e, h=h, zb=zb: e.scalar_tensor_tensor(out=Dst[:, h, :], in0=Dst[:, h, :], scalar=ev[1 - par][:, h:h + 1], in1=zb[:, 0:257], op0=ALU.mult, op1=ALU.add),
                             banks=[ZB], reads=[EV[1 - par]], writes=[DST[h]])
                    S.op("pool", lambda e, h=h: e.tensor_scalar(out=Cbf[:, h, :], in0=Dst[:, h, :], scalar1=ev[par][:, h:h + 1], scalar2=0.0, op0=ALU.mult, op1=ALU.add), reads=[DST[h], EV[par]], writes=[CBF[h]])

            def emit_chains():
                HSB = HS[0]
                for h in range(4):
                    PP_, PPB = pbanks[h]
                    S.op("dve", lambda e, PP_=PP_, h=h: e.tensor_tensor(out=hs[:n, 0, h:h + 1], in0=PP_[:n, 256:257], in1=wv[:n, h:h + 1], op=ALU.mult), banks=[PPB], reads=[WV], writes=[HSB])
                S.op("dve", lambda e: e.scalar_tensor_tensor(out=hs[:n, 1, :], in0=hs[:n, 0, :], scalar=-1.0, in1=hs[:n, 0, :], op0=ALU.mult, op1=ALU.max), reads=[HSB], writes=[HSB])
                S.op("dve", lambda e: e.tensor_scalar(out=hs[:n, 1, :], in0=hs[:n, 1, :], scalar1=1.0, scalar2=None, op0=ALU.max), reads=[HSB], writes=[HSB])
                S.op("dve", lambda e: e.reciprocal(out=hs[:n, 2, :], in_=hs[:n, 1, :]), reads=[HSB], writes=[HSB])
                S.op("dve", lambda e: e.tensor_tensor(out=hs[:n, 3, :], in0=hs[:n, 2, :], in1=wv[:n, :], op=ALU.mult), reads=[HSB, WV], writes=[HSB])
                for h in range(4):
                    PP_, PPB = pbanks[h]
                    S.op("act", lambda e, PP_=PP_, h=h: e.activation(out=junk[:n, h * 256:(h + 1) * 256], in_=PP_[:n, 0:256], func=AF.Square, scale=hs[:n, 3, h:h + 1], accum_out=hs[:n, 4, h:h + 1]),
                         banks=[PPB], reads=[HSB], writes=[JUNK[h // 2], HS[1]])
                S.op("pool", lambda e: e.tensor_scalar(out=hs[:n, 5, :], in0=hs[:n, 4, :], scalar1=1.0 / 256, scalar2=EPS, op0=ALU.mult, op1=ALU.add), reads=[HS[1]], writes=[HS[2]])
                S.op("pool", lambda e: e.tensor_tensor(out=hs[:n, 5, :], in0=hs[:n, 5, :], in1=neghalf[:n, 0:4], op=ALU.pow), reads=[HS[2], CONSTB], writes=[HS[2]])
                S.op("dve", lambda e: e.scalar_tensor_tensor(out=hs[:n, 6, :], in0=hs[:n, 5, :], scalar=0.25, in1=hs[:n, 3, :], op0=ALU.mult, op1=ALU.mult), reads=[HS[2], HSB], writes=[HS[3]])
                for h in range(4):
                    PP_, PPB = pbanks[h]
                    S.op("dve", lambda e, PP_=PP_, h=h: e.scalar_tensor_tensor(out=y[:n, 1024 + h * 256:1024 + (h + 1) * 256], in0=PP_[:n, 0:256], scalar=hs[:n, 6, h:h + 1], in1=to[:n, h * 256:(h + 1) * 256], op0=ALU.mult, op1=ALU.mult),
                         banks=[PPB], reads=[HS[3], TO[h // 2]], writes=[YM[h]])


            def front_staged():
                zbk, ZBK = zbank()
                rdk = [HT[0], HT[1]] + [WINB[(k, s_)] for k in range(8) for s_ in SEG_OF["kv"]]
                S.group("pe", [lambda e, k=k: e.matmul(out=zbk[:n, :264], lhsT=hT[:, k, :n], rhs=win[:, k, 1024:1288], start=(k == 0), stop=(k == 7)) for k in range(8)], reads=rdk, banks=[ZBK])
                S.op("dve", lambda e: e.tensor_copy(out=kvg[:n, :264], in_=zbk[:n, :264]), banks=[ZBK], writes=[KVG])
                qbk = [(pb, PB), (pc, PC)]
                for qi in range(2):
                    zb, ZB = qbk[qi]
                    rd = [HT[0], HT[1]] + [WINB[(k, s_)] for k in range(8) for s_ in SEG_OF["q%d" % qi]]
                    S.group("pe", [lambda e, k=k, zb=zb, qi=qi: e.matmul(out=zb[:n, :], lhsT=hT[:, k, :n], rhs=win[:, k, qi * 512:(qi + 1) * 512], start=(k == 0), stop=(k == 7)) for k in range(8)], reads=rd, banks=[ZB])
                    S.op("act", lambda e, zb=zb, qi=qi: e.activation(out=junk[:n, qi * 512:(qi + 1) * 512], in_=zb[:n, :], func=AF.Square), banks=[ZB], writes=[JUNK[qi]])
                S.op("act", lambda e: e.activation(out=ktok[:n, 0, :], in_=kvg[:n, 0:128], func=AF.Square), reads=[KVG], writes=[KTOK])
                S.op("dve", lambda e: e.tensor_tensor(out=gl[:n, :], in0=kvg[:n, 256:264], in1=bif[:n, l * 8:(l + 1) * 8], op=ALU.add), reads=[KVG, PAR], writes=[GL])
                for qi in range(2):
                    S.op("dve", lambda e, qi=qi: e.tensor_reduce(out=ssq[:n, 8 * qi:8 * qi + 8], in_=junk[:n, qi * 512:(qi + 1) * 512].rearrange("p (h d) -> p h d", h=8), axis=AX.X, op=ALU.add),
                         reads=[JUNK[qi]], writes=[SSQ[qi]])
                S.op("dve", lambda e: e.tensor_reduce(out=ssq[:n, 16:18], in_=ktok[:n, 0, :].rearrange("p (g d) -> p g d", g=2), axis=AX.X, op=ALU.add), reads=[KTOK], writes=[SSK])
                S.op("pool", lambda e: e.tensor_scalar(out=rq[:n, :], in0=ssq[:n, :], scalar1=1.0 / 64, scalar2=EPS, op0=ALU.mult, op1=ALU.add), reads=[SSQ[0], SSQ[1], SSK], writes=[RQ[0], RQ[1], RK])
                S.op("pool", lambda e: e.tensor_tensor(out=rq[:n, :], in0=rq[:n, :], in1=neghalf[:n, 0:18], op=ALU.pow), reads=[RQ[0], CONSTB], writes=[RQ[0], RQ[1], RK])
                S.op("act", lambda e: e.activation(out=gt[:n, :], in_=gl[:n, 4:8], func=AF.Exp, scale=-1.0), reads=[GL], writes=[GT])
                S.op("act", lambda e: e.activation(out=gt[:n, :], in_=gt[:n, :], func=AF.Ln, bias=1.0), reads=[GT], writes=[GT])
                S.op("act", lambda e: e.copy(out=vdst[:n, :, 0:64], in_=kvg[:n, 128:256].rearrange("p (g d) -> p g d", g=2)), reads=[KVG], writes=[VD])
                for qi in range(2):
                    zb, ZB = qbk[qi]
                    for h in range(8):
                        hh = 8 * qi + h
                        S.op("dve", lambda e, h=h, hh=hh, zb=zb: e.scalar_tensor_tensor(out=qb[:n, hh, :], in0=zb[:n, h * 64:(h + 1) * 64], scalar=rq[:n, hh:hh + 1], in1=gq[:n, :], op0=ALU.mult, op1=ALU.mult),
                             banks=[ZB], reads=[RQ[qi], PARL], writes=[QB[qi]])
                for g in range(2):
                    S.op("dve", lambda e, g=g: e.scalar_tensor_tensor(out=kb[:n, g, :], in0=kvg[:n, g * 64:(g + 1) * 64], scalar=rq[:n, 16 + g:17 + g], in1=gk[:n, :], op0=ALU.mult, op1=ALU.mult),
                         reads=[KVG, RK, PARL], writes=[KB])
                do_rope_all(n, par)

            T_ = f'L{l}T{j}:'
            seq = [('front', front_staged), ('z', lambda: fm_z(0)), ('z', lambda: fm_z(1)),
                   ('fmT0', lambda: fm_T(0)), ('z', lambda: zgroup(n, "mo0", 4360, 512, mk_ev_mo(0))),
                   ('fmT1', lambda: fm_T(1)), ('kv2', after_kv2), ('conv0', lambda: conv_half(0)),
                   ('z', lambda: zgroup(n, "ag0", 1288, 512, mk_ev_ag(0))),
                   ('conv1', lambda: conv_half(1)), ('kv2b', after_kv2b),
                   ('z', lambda: zgroup(n, "mv0", 3336, 512, mk_ev_mv(0))),
                   ('qkT', kT_fn), ('qkT', lambda: qT_fn(0)),
                   ('z', lambda: zgroup(n, "mo1", 4872, 512, mk_ev_mo(1))),
                   ('qkT', lambda: qT_fn(1)),
                   ('z', lambda: zgroup(n, "ag1", 1800, 512, mk_ev_ag(1))),
                   ('qkT', lambda: qT_fn(2)), ('qkT', lambda: qT_fn(3)),
                   ('attn0', lambda: attn_s(0)), ('z', lambda: zgroup(n, "mv1", 3848, 512, mk_ev_mv(1))), ('attn0', lambda: attn_pv(0)),
                   ('attn1', lambda: attn_s(1)), ('z', lambda: zgroup(n, "mg0", 5384, 512, mk_ev_mg(0))), ('attn1', lambda: attn_pv(1)),
                   ('attn2', lambda: attn_s(2)), ('z', lambda: zgroup(n, "mg1", 5896, 512, mk_ev_mg(1))), ('attn2', lambda: attn_pv(2)),
                   ('attn3', lambda: attn_s(3)), ('ml_a', ml_a), ('attn3', lambda: attn_pv(3)),
                   ('pre', (lambda: pre_fn(*nxt)) if nxt is not None else (lambda: None)),
                   ('ml_P', ml_P), ('ml_C', ml_C)]
            for tg, fn in seq:
                S.tag = T_ + tg
                fn()

            S.tag = f'L{l}T{j}:pre'
            S.tag = f'L{l}T{j}:out'
            obanks = [(pa, PA), cbanks[0]]
            for part in range(2):
                if part == 1:
                    S.tag = f'L{l}T{j}:chains'
                    emit_chains()
                    if nxt is not None:
                        S.tag = f'L{l}T{j}:ht'
                        ht_fn(*nxt)
                    S.tag = f'L{l}T{j}:out1'
                for q4 in (2 * part, 2 * part + 1):
                    tb, TB = tbank()
                    S.group("pe", [lambda e, i=i, tb=tb: e.transpose(out=tb[:, i * 128:i * 128 + n], in_=y[:n, (q4 * 4 + i) * 128:(q4 * 4 + i + 1) * 128], identity=ident_bf[:n, :n]) for i in range(4)],
                            reads=(YA if part == 0 else YM) + [CONSTB], banks=[TB])
                    src = tb[:, 0:512].rearrange("p (c t) -> p c t", c=4)[:, :, :n]
                    if q4 % 2 == 0:
                        S.op("act", lambda e, src=src, q4=q4: e.copy(out=yT[:, q4 * 4:q4 * 4 + 4, :n], in_=src), banks=[TB], writes=[YT[q4]])
                    else:
                        S.op("dve", lambda e, src=src, q4=q4: e.tensor_copy(out=yT[:, q4 * 4:q4 * 4 + 4, :n], in_=src), banks=[TB], writes=[YT[q4]])
                for g in range(2):
                    zb, ZB = obanks[g]
                    S.group("pe", [lambda e, k=k, zb=zb: e.matmul(out=zb[:n, :], lhsT=yT[:, k, :n], rhs=wout[:, k, g * 512:(g + 1) * 512], start=(k == 0), stop=(k == 15)) for k in range(8 * part, 8 * part + 8)],
                            reads=YT[2 * part:2 * part + 2] + [WOUTB[(k, g)] for k in range(8 * part, 8 * part + 8)], banks=[ZB])
            for g in range(2):
                zb, ZB = obanks[g]
                S.op("dve", lambda e, zb=zb, g=g: e.tensor_tensor(out=xt_[:n, g * 512:(g + 1) * 512], in0=zb[:n, :], in1=xt_[:n, g * 512:(g + 1) * 512], op=ALU.add), banks=[ZB], writes=[X])
            if last_layer:
                if j > 0:
                    S.dma(y_d[(j - 1) * 128:j * 128, :], xt_[:, :], reads=[X], writes=[YOUT])
            else:
                S.dma(xs_d[j, :n, :], xt_[:n, :], reads=[X], writes=[XS[j]])

        YOUT = Buf("yout")
        for l in range(depth):
            load_x(l, 0)
            load_layer_weights(l)
            pre_fn(l, 0)
            ht_fn(l, 0)
            for j in range(NT):
                tile_fn(l, j, l == depth - 1, (l, j + 1) if j + 1 < NT else None)
        S.drain_dmas("sp")
        print("ninst", S.ninst, "nwaits", S.nwaits)
        build.last_log = S.log
    return nc


def host_consts():
    idx = np.arange(128)
    ident = np.eye(128, dtype=np.float32)
    tri = (idx[:, None] <= idx[None, :]).astype(np.float32)
    ones = np.ones((128, 128), np.float32)
    mcur = np.where(idx[:, None] <= idx[None, :], 0.0, NEGM).astype(np.float32)
    mprev = np.where(idx[:, None] > idx[None, :], 0.0, NEGM).astype(np.float32)
    return np.concatenate([ident, tri, ones, mcur, mprev], axis=1)


def rope_table(T):
    NT = T + 1
    L = N_META + T * 128
    pos = np.arange(L, dtype=np.float32)
    inv_freq = (np.float32(500000.0) ** (-np.arange(0, 16, 2, dtype=np.float32) / np.float32(16))).astype(np.float32)
    ang = (pos[:, None] * inv_freq[None, :]).astype(np.float32)
    c_, s_ = np.cos(ang), np.sin(ang)
    cs = np.concatenate([c_, c_, -s_, s_], axis=1).astype(np.float32)
    out = np.zeros((NT, 128, 32), np.float32)
    out[0, :16] = cs[:16]
    out[1:] = cs[16:].reshape(T, 128, 32)
    return out


def prep_inputs(inp, T, depth, batch_sel):
    f = lambda a: np.ascontiguousarray(np.asarray(a, dtype=np.float32))
    rep = lambda a: np.ascontiguousarray(np.broadcast_to(f(a).reshape(1, -1), (128, f(a).size)))
    fm = lambda a: np.ascontiguousarray(f(a).reshape(depth, 8, 128).transpose(2, 0, 1).reshape(128, depth * 8))
    common = {
        "meta": f(inp["meta"]),
        "w_in": f(inp["w_in"]),
        "w_out": f(inp["w_out"]),
        "ng_fm": fm(inp["norm_g"]),
        "og_fm": fm(inp["mlstm_out_norm_g"]),
        "gq_b": rep(inp["attn_q_norm_g"]),
        "gk_b": rep(inp["attn_k_norm_g"]),
        "sink_b": rep(inp["attn_sink"]),
        "convw_fm": np.ascontiguousarray(f(inp["mlstm_conv_w"]).reshape(depth, 4, 8, 128).transpose(3, 0, 1, 2).reshape(128, depth * 32)),
        "convb_row": f(inp["mlstm_conv_b"]).reshape(1, depth * 1024),
        "bif_b": rep(np.concatenate([f(inp["mlstm_b_i"]), f(inp["mlstm_b_f"])], axis=1)),
        "rope": rope_table(T),
        "cst": host_consts(),
    }
    x = f(inp["x"])
    maps = []
    for b in batch_sel:
        m = dict(common)
        m["x"] = np.ascontiguousarray(x[b])
        maps.append(m)
    return maps


_NC_CACHE = {}


def kernel(**inputs):
    x = np.asarray(inputs["x"])
    B, SEQ, _ = x.shape
    T = SEQ // 128
    depth = np.asarray(inputs["w_in"]).shape[0]
    key = (T, depth)
    if key not in _NC_CACHE:
        _NC_CACHE[key] = build(T, depth)
    nc = _NC_CACHE[key]
    n_cores = 8
    active = [0, 1, 4, 5]
    real = prep_inputs(inputs, T, depth, list(range(B)))
    zero = {k: np.zeros_like(v) for k, v in real[0].items()}
    maps = [zero] * n_cores
    maps = list(maps)
    for b, c in enumerate(active[:B]):
        maps[c] = real[b]
    res = run_bass_kernel_spmd(nc, maps, core_ids=list(range(n_cores)))
    out = np.stack([np.asarray(res.results[active[b]]["y"]) for b in range(B)], axis=0)
    return out.astype(np.float32)
```
